# Optimizing a Trainium2 kernel written in Bass

```python
import math
import jax, jax.numpy as jnp
from jax import lax
import numpy as np

D_MODEL = 4096
BATCH = 4
SEQ = 4096
DEPTH = 1
DEC_BATCH = 2
DEC_SEQ = 8192
PAST_LEN = 128

H_DIFF = D_MODEL // 256
DH_DIFF = 64
V_DIFF = 2 * DH_DIFF
H_MLA = D_MODEL // 256
Q_LORA = 1024
KV_LORA = 512
QK_NOPE = 128
QK_ROPE = 64
V_MLA = 128
ROPE_THETA = 10000.0
N_BUCKETS = 32
MAX_DISTANCE = 128
D_FF = 11008
CONV_WIDTH = 3
Q_BLOCK = 128
EPS = 1e-6

DIFF_Q_COLS = H_DIFF * 2 * DH_DIFF
DIFF_K_COLS = H_DIFF * 2 * DH_DIFF
DIFF_V_COLS = H_DIFF * V_DIFF
MLA_V_COLS = H_MLA * V_MLA
GATE_COLS = 2 * D_MODEL
IN_COLS = DIFF_Q_COLS + DIFF_K_COLS + DIFF_V_COLS + Q_LORA + KV_LORA + QK_ROPE + GATE_COLS
IN_SPLITS = (
    DIFF_Q_COLS,
    DIFF_Q_COLS + DIFF_K_COLS,
    DIFF_Q_COLS + DIFF_K_COLS + DIFF_V_COLS,
    DIFF_Q_COLS + DIFF_K_COLS + DIFF_V_COLS + Q_LORA,
    DIFF_Q_COLS + DIFF_K_COLS + DIFF_V_COLS + Q_LORA + KV_LORA,
    DIFF_Q_COLS + DIFF_K_COLS + DIFF_V_COLS + Q_LORA + KV_LORA + QK_ROPE,
)

kernel_name = 'hybrid_diffattn_mla_convffn_encoder'


def rmsnorm(x, g):
    xf = x.astype(jnp.float32)
    y = xf * lax.rsqrt(jnp.mean(xf * xf, axis=-1, keepdims=True) + EPS) * g.astype(jnp.float32)
    return y.astype(x.dtype)


def t5_bucket(rel):
    nb = N_BUCKETS // 2
    max_exact = nb // 2
    ret = (rel > 0).astype(jnp.int32) * nb
    n = jnp.abs(rel)
    nf = jnp.maximum(n, max_exact).astype(jnp.float32)
    large = max_exact + (jnp.log(nf / max_exact) / math.log(MAX_DISTANCE / max_exact)
                         * (nb - max_exact)).astype(jnp.int32)
    large = jnp.minimum(large, nb - 1)
    return ret + jnp.where(n < max_exact, n, large)


def to_blocks(t):
    b, s = t.shape[0], t.shape[1]
    return t.reshape((b, s // Q_BLOCK, Q_BLOCK) + t.shape[2:]).swapaxes(0, 1)


def from_blocks(t):
    n, b, q = t.shape[0], t.shape[1], t.shape[2]
    return t.swapaxes(0, 1).reshape((b, n * q) + t.shape[3:])


def rope(x, cos, sin):
    xf = x.astype(jnp.float32)
    x1, x2 = jnp.split(xf, 2, axis=-1)
    return jnp.concatenate([x1 * cos - x2 * sin, x1 * sin + x2 * cos], axis=-1).astype(x.dtype)


def diff_attention(q, k, v, lam, rel_bias):
    s_len = k.shape[1]
    scale = DH_DIFF ** -0.5
    kpos = jnp.arange(s_len, dtype=jnp.int32)
    qb = to_blocks(q)
    nblk = qb.shape[0]

    def one(args):
        qi, bi = args
        qpos = bi * Q_BLOCK + jnp.arange(Q_BLOCK, dtype=jnp.int32)
        bias = rel_bias[t5_bucket(kpos[None, :] - qpos[:, None])]
        bias = bias.astype(jnp.float32).transpose(2, 0, 1)
        s = jnp.einsum('bqhcd,bkhcd->bhcqk', qi, k).astype(jnp.float32) * scale
        p = jax.nn.softmax(s + bias[None, :, None], axis=-1)
        a = p[:, :, 0] - lam * p[:, :, 1]
        return jnp.einsum('bhqk,bkhe->bqhe', a.astype(v.dtype), v)

    out = lax.map(one, (qb, jnp.arange(nblk, dtype=jnp.int32)))
    return from_blocks(out)


def mla_attention(q_nope, q_pe, k_nope, k_pe, v):
    scale = (QK_NOPE + QK_ROPE) ** -0.5

    def one(args):
        qn, qp = args
        s = (jnp.einsum('bqhd,bkhd->bhqk', qn, k_nope)
             + jnp.einsum('bqhr,bkr->bhqk', qp, k_pe)).astype(jnp.float32) * scale
        p = jax.nn.softmax(s, axis=-1)
        return jnp.einsum('bhqk,bkhe->bqhe', p.astype(v.dtype), v)

    out = lax.map(one, (to_blocks(q_nope), to_blocks(q_pe)))
    return from_blocks(out)


def depthwise_conv3(h, w, b):
    hp = jnp.pad(h, ((0, 0), (1, 1), (0, 0)))
    return hp[:, :-2] * w[0] + hp[:, 1:-1] * w[1] + hp[:, 2:] * w[2] + b


def encoder_layer(x, layer_idx, rel_bias, rms_attn_g, w_in, lambda_q1, lambda_k1, lambda_q2,
                  lambda_k2, diff_subln_g, mla_q_norm_g, w_mla_q_up, mla_kv_norm_g, w_mla_kv_up,
                  w_branch_a, w_branch_b, w_out, rms_ffn_g, w_ffn_up, conv_w, conv_b, w_ffn_down):
    b, s_len, _ = x.shape
    h = rmsnorm(x, rms_attn_g)
    proj = h @ w_in
    dq, dk, dv, q_lat, kv_lat, k_pe, gates = jnp.split(proj, IN_SPLITS, axis=-1)

    lambda_init = 0.8 - 0.6 * math.exp(-0.3 * layer_idx)
    f32 = jnp.float32
    lam = (jnp.exp(jnp.sum(lambda_q1.astype(f32) * lambda_k1.astype(f32)))
           - jnp.exp(jnp.sum(lambda_q2.astype(f32) * lambda_k2.astype(f32))) + lambda_init)
    a_out = diff_attention(dq.reshape(b, s_len, H_DIFF, 2, DH_DIFF),
                           dk.reshape(b, s_len, H_DIFF, 2, DH_DIFF),
                           dv.reshape(b, s_len, H_DIFF, V_DIFF), lam, rel_bias)
    a_out = rmsnorm(a_out, diff_subln_g) * (1.0 - lambda_init)
    a_out = a_out.astype(x.dtype).reshape(b, s_len, DIFF_V_COLS)

    q = (rmsnorm(q_lat, mla_q_norm_g) @ w_mla_q_up).reshape(b, s_len, H_MLA, QK_NOPE + QK_ROPE)
    q_nope, q_pe = q[..., :QK_NOPE], q[..., QK_NOPE:]
    kv = (rmsnorm(kv_lat, mla_kv_norm_g) @ w_mla_kv_up).reshape(b, s_len, H_MLA, QK_NOPE + V_MLA)
    k_nope, v_mla = kv[..., :QK_NOPE], kv[..., QK_NOPE:]
    pos = jnp.arange(s_len, dtype=jnp.float32)
    inv_freq = ROPE_THETA ** (-jnp.arange(0, QK_ROPE, 2, dtype=jnp.float32) / QK_ROPE)
    ang = pos[:, None] * inv_freq[None, :]
    cos, sin = jnp.cos(ang), jnp.sin(ang)
    q_pe = rope(q_pe, cos[None, :, None, :], sin[None, :, None, :])
    k_pe = rope(k_pe, cos[None], sin[None])
    b_out = mla_attention(q_nope, q_pe, k_nope, k_pe, v_mla).reshape(b, s_len, MLA_V_COLS)

    g_a, g_b = jnp.split(gates, 2, axis=-1)
    merged = jax.nn.sigmoid(g_a) * (a_out @ w_branch_a) + jax.nn.sigmoid(g_b) * (b_out @ w_branch_b)
    x = x + merged @ w_out

    h2 = rmsnorm(x, rms_ffn_g)
    u = depthwise_conv3(h2 @ w_ffn_up, conv_w, conv_b)
    gate, up = jnp.split(u, 2, axis=-1)
    x = x + (jax.nn.silu(gate) * up) @ w_ffn_down
    return x


def setup_inputs(seed: int = 0) -> dict:
    key = jax.random.key(seed)
    ks = jax.random.split(key, 24)
    f32 = jnp.float32
    L = DEPTH

    def nrm(k, shape, fan_in):
        return jax.random.normal(k, shape, f32) * (fan_in ** -0.5)

    def gain(k, shape):
        return 1.0 + 0.02 * jax.random.normal(k, shape, f32)

    return {
        'x_prompt': jax.random.normal(ks[0], (BATCH, SEQ, D_MODEL), f32),
        'x_sample': jax.random.normal(ks[1], (DEC_BATCH, DEC_SEQ, D_MODEL), f32),
        'rel_bias': 0.5 * jax.random.normal(ks[2], (N_BUCKETS, H_DIFF), f32),
        'final_norm_g': gain(ks[3], (D_MODEL,)),
        'rms_attn_g': gain(ks[4], (L, D_MODEL)),
        'w_in': nrm(ks[5], (L, D_MODEL, IN_COLS), D_MODEL),
        'lambda_q1': 0.1 * jax.random.normal(ks[6], (L, DH_DIFF), f32),
        'lambda_k1': 0.1 * jax.random.normal(ks[7], (L, DH_DIFF), f32),
        'lambda_q2': 0.1 * jax.random.normal(ks[8], (L, DH_DIFF), f32),
        'lambda_k2': 0.1 * jax.random.normal(ks[9], (L, DH_DIFF), f32),
        'diff_subln_g': gain(ks[10], (L, V_DIFF)),
        'mla_q_norm_g': gain(ks[11], (L, Q_LORA)),
        'w_mla_q_up': nrm(ks[12], (L, Q_LORA, H_MLA * (QK_NOPE + QK_ROPE)), Q_LORA),
        'mla_kv_norm_g': gain(ks[13], (L, KV_LORA)),
        'w_mla_kv_up': nrm(ks[14], (L, KV_LORA, H_MLA * (QK_NOPE + V_MLA)), KV_LORA),
        'w_branch_a': nrm(ks[15], (L, DIFF_V_COLS, D_MODEL), DIFF_V_COLS),
        'w_branch_b': nrm(ks[16], (L, MLA_V_COLS, D_MODEL), MLA_V_COLS),
        'w_out': nrm(ks[17], (L, D_MODEL, D_MODEL), D_MODEL),
        'rms_ffn_g': gain(ks[18], (L, D_MODEL)),
        'w_ffn_up': nrm(ks[19], (L, D_MODEL, 2 * D_FF), D_MODEL),
        'conv_w': nrm(ks[20], (L, CONV_WIDTH, 2 * D_FF), CONV_WIDTH),
        'conv_b': 0.02 * jax.random.normal(ks[21], (L, 2 * D_FF), f32),
        'w_ffn_down': nrm(ks[22], (L, D_FF, D_MODEL), D_FF),
    }


def reference(x_prompt, x_sample, rel_bias, final_norm_g, rms_attn_g, w_in, lambda_q1, lambda_k1,
              lambda_q2, lambda_k2, diff_subln_g, mla_q_norm_g, w_mla_q_up, mla_kv_norm_g,
              w_mla_kv_up, w_branch_a, w_branch_b, w_out, rms_ffn_g, w_ffn_up, conv_w, conv_b,
              w_ffn_down):
    def trunk(x):
        for l in range(DEPTH):
            x = encoder_layer(x, l, rel_bias, rms_attn_g[l], w_in[l], lambda_q1[l], lambda_k1[l],
                              lambda_q2[l], lambda_k2[l], diff_subln_g[l], mla_q_norm_g[l],
                              w_mla_q_up[l], mla_kv_norm_g[l], w_mla_kv_up[l], w_branch_a[l],
                              w_branch_b[l], w_out[l], rms_ffn_g[l], w_ffn_up[l], conv_w[l],
                              conv_b[l], w_ffn_down[l])
        return rmsnorm(x, final_norm_g)

    y_prompt = trunk(x_prompt)
    y_sample = trunk(x_sample)
    return (y_prompt, y_sample)
```

```python
import math
import numpy as np
import concourse.bass as bass
import concourse.mybir as mybir
from concourse.bass_utils import run_bass_kernel_spmd
from contextlib import ExitStack

F32 = mybir.dt.float32
BF16 = mybir.dt.bfloat16
ALU = mybir.AluOpType
AF = mybir.ActivationFunctionType

D = 4096
H = 16
T = 456
NTI = 9
WIN = NTI * T
NOTH = 4096
NK = 8192
IN_COLS = 15936
DFF = 11008
EPS = 1e-6
NEG = -30000.0
U0 = 545
WT = 1218
WB = WT + 2 * T
SC_DIFF = 64 ** -0.5
SC_MLA = 192 ** -0.5


class Sched:
    def __init__(self, nc):
        self.nc = nc
        self.E = {"pe": nc.tensor, "act": nc.scalar, "dve": nc.vector,
                  "pool": nc.gpsimd, "sp": nc.sync}
        self.sem = {e: nc.alloc_semaphore(name="prog_" + e) for e in self.E}
        self.cnt = {e: 0 for e in self.E}
        self.known = {e: {} for e in self.E}
        self.last_w = {}
        self.readers = {}
        self.chan = {}
        self.pe_pr = set()
        self.pe_pw = set()
        self.stack = ExitStack()
        self.nwait = 0
        self.nins = 0
        self._uid = 0

    def sb(self, name, shape, dt, stack=None):
        return (stack or self.stack).enter_context(self.nc.sbuf_tensor("s_" + name, list(shape), dt))

    def ps(self, name, shape, dt=F32):
        return self.stack.enter_context(self.nc.psum_tensor(name, list(shape), dt))

    def _semof(self, key):
        if isinstance(key, tuple):
            return self.chan[key[1]][0]
        return self.sem[key]

    def _deps(self, eng, reads, writes):
        toks = []
        for b in reads:
            if eng != "pe" and b in self.pe_pw:
                raise RuntimeError(f"buffer {b} has pending unsignalled PE write")
            t = self.last_w.get(b)
            if t is not None:
                toks.append(t)
        for b in writes:
            if eng != "pe" and (b in self.pe_pw or b in self.pe_pr):
                raise RuntimeError(f"buffer {b} has pending unsignalled PE access")
            t = self.last_w.get(b)
            if t is not None:
                toks.append(t)
            toks.extend(self.readers.get(b, ()))
        need = {}
        for key, val in toks:
            if key == "pe" and eng == "pe":
                continue
            if need.get(key, 0) < val:
                need[key] = val
        e = self.E[eng]
        kn = self.known[eng]
        for key, val in need.items():
            if kn.get(key, 0) < val:
                e.wait_ge(self._semof(key), val)
                kn[key] = val
                self.nwait += 1

    def _register(self, tok, reads, writes):
        for b in reads:
            self.readers.setdefault(b, []).append(tok)
        for b in writes:
            self.last_w[b] = tok
            self.readers[b] = []

    def op(self, eng, fn, reads=(), writes=(), sig=True):
        self._deps(eng, reads, writes)
        ins = fn()
        self.nins += 1
        if eng == "pe" and not sig:
            self.pe_pr.update(reads)
            self.pe_pw.update(writes)
            return ins
        self.cnt[eng] += 1
        ins.then_inc(self.sem[eng], 1)
        tok = (eng, self.cnt[eng])
        if eng == "pe":
            reads = set(reads) | self.pe_pr
            writes = set(writes) | self.pe_pw
            self.pe_pr = set()
            self.pe_pw = set()
        self._register(tok, reads, writes)
        return ins

    def dma(self, eng, out, in_, reads=(), writes=(), chan=None, **kw):
        if chan is None:
            chan = writes[0] if writes else reads[0]
        if chan not in self.chan:
            self.chan[chan] = [self.nc.alloc_semaphore(name="dch%d" % len(self.chan)), 0]
        self._deps(eng, reads, writes)
        c = self.chan[chan]
        ins = self.E[eng].dma_start(out=out, in_=in_, **kw)
        ins.then_inc(c[0], 16)
        c[1] += 16
        self.nins += 1
        tok = (("dma", chan), c[1])
        self._register(tok, reads, writes)
        return ins

    def barrier(self):
        for eng, e in self.E.items():
            kn = self.known[eng]
            for chan, (sem, val) in self.chan.items():
                key = ("dma", chan)
                if val > 0 and kn.get(key, 0) < val:
                    e.wait_ge(sem, val)
                    kn[key] = val
            for k in self.E:
                if k != eng and self.cnt[k] > 0 and kn.get(k, 0) < self.cnt[k]:
                    e.wait_ge(self.sem[k], self.cnt[k])
                    kn[k] = self.cnt[k]
        self.last_w = {}
        self.readers = {}

    def finish(self, eng="sp"):
        e = self.E[eng]
        for chan, (sem, val) in self.chan.items():
            if val > 0 and self.known[eng].get(("dma", chan), 0) < val:
                e.wait_ge(sem, val)
        for k in self.E:
            if k != eng and self.cnt[k] > 0:
                e.wait_ge(self.sem[k], self.cnt[k])


def blocks_of(n):
    out = []
    o = 0
    while o < n:
        out.append((o, min(128, n - o)))
        o += 128
    return out


def build_program(cfg=None):
    cfg = cfg or {}
    phases = cfg.get("phases", "WABCD")
    n_own_tiles = cfg.get("n_own_tiles", NTI)
    n_oth_tiles = cfg.get("n_oth_tiles", 9)
    heads = cfg.get("heads", list(range(H)))
    nkt_own = cfg.get("nkt_own", 32)
    nkt_oth = cfg.get("nkt_oth", 32)
    dump = cfg.get("dump", ())

    nc = bass.Bass("TRN2", target_bir_lowering=False)
    S = Sched(nc)

    def din(name, shape, dt=F32):
        return nc.dram_tensor(name, list(shape), dt, kind="ExternalInput").ap()

    def dscr(name, shape, dt=BF16):
        kind = "ExternalOutput" if name in dump else "Internal"
        return nc.dram_tensor(name, list(shape), dt, kind=kind).ap()

    xw = din("xw", [WIN, D])
    xo = din("xo", [NOTH, D])
    cs1w = din("cs1w", [64, WIN]); cs2w = din("cs2w", [64, WIN])
    cs1o = din("cs1o", [64, NOTH]); cs2o = din("cs2o", [64, NOTH])
    bidx_d = din("bidx", [128, WB])
    cvals_d = din("cvals", [128, 40])
    ident_d = din("ident", [128, 128])
    rel_bias = din("rel_bias", [32, 16])
    final_norm_g = din("final_norm_g", [D])
    rms_attn_g = din("rms_attn_g", [1, D])
    w_in = din("w_in", [1, D, IN_COLS])
    lam_d = [din(n, [1, 64]) for n in ("lambda_q1", "lambda_k1", "lambda_q2", "lambda_k2")]
    diff_subln_g = din("diff_subln_g", [1, 128])
    mla_q_norm_g = din("mla_q_norm_g", [1, 1024])
    w_mla_q_up = din("w_mla_q_up", [1, 1024, 3072])
    mla_kv_norm_g = din("mla_kv_norm_g", [1, 512])
    w_mla_kv_up = din("w_mla_kv_up", [1, 512, 4096])
    w_branch_a = din("w_branch_a", [1, 2048, D])
    w_branch_b = din("w_branch_b", [1, 2048, D])
    w_out = din("w_out", [1, D, D])
    rms_ffn_g = din("rms_ffn_g", [1, D])
    w_ffn_up = din("w_ffn_up", [1, D, 2 * DFF])
    conv_w = din("conv_w", [1, 3, 2 * DFF])
    conv_b = din("conv_b", [1, 2 * DFF])
    w_ffn_down = din("w_ffn_down", [1, DFF, D])
    y_out = nc.dram_tensor("y", [4096, D], F32, kind="ExternalOutput").ap()

    QT = dscr("QT", [H, 128, WIN]); KT = dscr("KT", [H, 128, NK]); VV = dscr("VV", [NK, 2048])
    QN = dscr("QN", [H, 128, WIN]); QPE = dscr("QPE", [64, H, WIN])
    KN = dscr("KN", [H, 128, NK]); KPE = dscr("KPE", [64, NK]); VM = dscr("VM", [NK, 2048])
    SG = dscr("SG", [64, 128, WIN])
    AO = dscr("AO", [H, 128, WIN]); BO = dscr("BO", [H, 128, WIN])
    BT = dscr("BT", [H, 128, WB], F32)
    X1 = dscr("X1", [WIN, D], F32)
    X2 = dscr("X2", [NTI * 464, D], F32)

    wsc = {}
    wconv = []

    def add_wblock(key, src2d, r0, kc, c0, width, dst=None, dcol=0, dwidth=None):
        if dst is None:
            dst = dscr("ws_" + "_".join(str(k) for k in key), [128, kc, dwidth or width])
            wsc[key] = dst
        for cg in range(0, kc, 8):
            n = min(8, kc - cg)
            src = src2d[r0 + cg * 128:r0 + (cg + n) * 128, c0:c0 + width].rearrange("(c p) m -> p c m", p=128)
            wconv.append((dst[:, cg:cg + n, dcol:dcol + width], src, key))
        return dst

    win2 = w_in[0]
    for j in range(4):
        add_wblock(("dq", j), win2, 0, 32, 512 * j, 512)
    for j in range(4):
        add_wblock(("dk", j), win2, 0, 32, 2048 + 512 * j, 512)
    for j in range(4):
        add_wblock(("dv", j), win2, 0, 32, 4096 + 512 * j, 512)
    for j in range(2):
        add_wblock(("ql", j), win2, 0, 32, 6144 + 512 * j, 512)
    add_wblock(("kvl", 0), win2, 0, 32, 7168, 512)
    add_wblock(("kpe", 0), win2, 0, 32, 7680, 64)
    for j in range(16):
        add_wblock(("gate", j), win2, 0, 32, 7744 + 512 * j, 512)
    for j in range(8):
        add_wblock(("qup", j), w_mla_q_up[0], 0, 8, 384 * j, 384)
    for j in range(8):
        add_wblock(("kvup", j), w_mla_kv_up[0], 0, 4, 512 * j, 512)
    for j in range(8):
        add_wblock(("wa", j), w_branch_a[0], 0, 16, 512 * j, 512)
        add_wblock(("wb", j), w_branch_b[0], 0, 16, 512 * j, 512)
    for j in range(8):
        add_wblock(("wo", j), w_out[0], 0, 32, 512 * j, 512)
    for j in range(86):
        dst = add_wblock(("up", j), w_ffn_up[0], 0, 32, 128 * j, 128, dwidth=256)
        add_wblock(("up", j), w_ffn_up[0], 0, 32, DFF + 128 * j, 128, dst=dst, dcol=128)
    NCG_DOWN = 11
    for n in range(8):
        for cg in range(NCG_DOWN):
            kc = min(8, 86 - cg * 8)
            add_wblock(("down", n, cg), w_ffn_down[0], cg * 1024, kc, 512 * n, 512)

    pb = [S.ps("pb%d" % k, [128, 512]) for k in range(8)]
    pbk = [("pb", k) for k in range(8)]
    ident = S.sb("ident", [128, 128], F32)
    ones_f = S.sb("ones_f", [128, 128], F32)
    ones_b = S.sb("ones_b", [128, 128], BF16)
    cvals = S.sb("cvals", [128, 40], F32)
    rbrep = S.sb("rbrep", [128, 33 * 16], F32)
    coth = S.sb("coth", [128, 16], F32)
    neglam = S.sb("neglam", [128, 1], F32)
    gsub = S.sb("gsub", [128, 1], F32)
    epsv = S.sb("epsv", [128, 1], F32)
    g_attn = S.sb("g_attn", [128, 32], F32)
    g_ffn = S.sb("g_ffn", [128, 32], F32)
    g_q = S.sb("g_q", [128, 8], F32)
    g_kv = S.sb("g_kv", [128, 4], F32)
    cw = S.sb("cw", [128, 3, 172], F32)
    cb = S.sb("cb", [128, 172], F32)
    ssq = S.sb("ssq", [128, 8], F32)
    rs = S.sb("rs", [128, 8], F32)

    S.dma("sp", ident[:], ident_d, writes=["ident"], chan="pro")
    S.dma("sp", cvals[:], cvals_d, writes=["cvals"], chan="pro")
    S.dma("sp", rbrep[:, 0:512], rel_bias.rearrange("b h -> (b h)").partition_broadcast(128), writes=["rbrep"], chan="pro")
    S.op("dve", lambda: nc.vector.memset(rbrep[:, 512:528], NEG), writes=["rbrep2"])
    S.op("dve", lambda: nc.vector.memset(ones_f[:], 1.0), writes=["ones_f"])
    S.op("dve", lambda: nc.vector.memset(ones_b[:], 1.0), writes=["ones_b"])
    S.op("dve", lambda: nc.vector.memset(epsv[:], EPS), writes=["epsv"])
    S.dma("sp", g_attn[:], rms_attn_g[0].rearrange("(c p) -> p c", p=128), writes=["g_attn"], chan="pro", allow_slow_non_contiguous=True)
    S.dma("sp", g_ffn[:], rms_ffn_g[0].rearrange("(c p) -> p c", p=128), writes=["g_ffn"], chan="pro", allow_slow_non_contiguous=True)
    S.dma("sp", g_q[:], mla_q_norm_g[0].rearrange("(c p) -> p c", p=128), writes=["g_q"], chan="pro", allow_slow_non_contiguous=True)
    S.dma("sp", g_kv[:], mla_kv_norm_g[0].rearrange("(c p) -> p c", p=128), writes=["g_kv"], chan="pro", allow_slow_non_contiguous=True)
    S.dma("sp", gsub[:], diff_subln_g[0].rearrange("(p o) -> p o", o=1), writes=["gsub"], chan="pro", allow_slow_non_contiguous=True)
    for k in range(3):
        S.dma("sp", cw[:, k, :], conv_w[0, k].rearrange("(c p) -> p c", p=128), writes=[("cw", k)], chan="pro", allow_slow_non_contiguous=True)
    S.dma("sp", cb[:], conv_b[0].rearrange("(c p) -> p c", p=128), writes=["cb"], chan="pro", allow_slow_non_contiguous=True)

    if "W" in phases:
        wsel = cfg.get("wkeys")
        for dst, src, key in wconv:
            if wsel is not None and key[0] not in wsel:
                continue
            S.dma("pool", dst, src, writes=[("ws", key)], chan=("wsch", key[0]))
        for dst, src, key in wconv:
            ch = ("wsch", key[0])
            if ch in S.chan:
                S.last_w[("ws", key)] = (("dma", ch), S.chan[ch][1])

    with ExitStack() as st0:
        lt = [S.sb("lamt%d" % k, [128, 64], F32, st0) for k in range(4)]
        lj = S.sb("lamj", [128, 64], F32, st0)
        ld = S.sb("lamd", [128, 2], F32, st0)
        for k in range(4):
            S.dma("sp", lt[k][:], lam_d[k][0].partition_broadcast(128), writes=[("lamt", k)], chan="pro")
        S.barrier()
        S.op("dve", lambda: nc.vector.tensor_scalar(out=gsub[:], in0=gsub[:], scalar1=0.8, scalar2=None, op0=ALU.mult),
             reads=["gsub"], writes=["gsub"])
        for k in range(2):
            S.op("dve", lambda: nc.vector.scalar_tensor_tensor(out=lj[:], in0=lt[2 * k][:], scalar=1.0, in1=lt[2 * k + 1][:],
                                                                op0=ALU.mult, op1=ALU.mult, accum_out=ld[:, k:k + 1]),
                 reads=[("lamt", 2 * k), ("lamt", 2 * k + 1)], writes=["lamj", ("lamd", k)])
        S.op("act", lambda: nc.scalar.activation(out=ld[:], in_=ld[:], func=AF.Exp), reads=[("lamd", 0), ("lamd", 1)], writes=[("lamd", 0), ("lamd", 1)])
        S.op("dve", lambda: nc.vector.tensor_tensor(out=neglam[:], in0=ld[:, 1:2], in1=ld[:, 0:1], op=ALU.subtract),
             reads=[("lamd", 0), ("lamd", 1)], writes=["neglam"])
        S.op("dve", lambda: nc.vector.tensor_scalar(out=neglam[:], in0=neglam[:], scalar1=-0.2, scalar2=None, op0=ALU.add),
             reads=["neglam"], writes=["neglam"])
        S.op("dve", lambda: nc.vector.memset(coth[:], 0.0), writes=["coth"])
        for b in range(33):
            S.op("dve", lambda: nc.vector.scalar_tensor_tensor(out=coth[:], in0=rbrep[:, 16 * b:16 * b + 16], scalar=cvals[:, b:b + 1],
                                                                in1=coth[:], op0=ALU.mult, op1=ALU.add),
                 reads=["rbrep", "rbrep2", "cvals", "coth"], writes=["coth"])
        S.barrier()

    if "B" in phases:
        with ExitStack() as st0:
            bidx = S.sb("bidx", [128, WB], F32, st0)
            btmp = S.sb("btmp", [128, WB], F32, st0)
            bacc = [S.sb("bacc%d" % k, [128, WB], F32, st0) for k in range(2)]
            S.dma("sp", bidx[:], bidx_d, writes=["bidx"], chan="pro")
            S.barrier()
            for hi, h in enumerate(heads):
                ba = bacc[hi % 2]
                bk = ("bacc", hi % 2)
                S.op("pool", lambda: nc.gpsimd.memset(ba[:], 0.0), writes=[bk])
                for b in range(33):
                    S.op("pool", lambda: nc.gpsimd.tensor_scalar(out=btmp[:], in0=bidx[:], scalar1=float(b), scalar2=rbrep[:, 16 * b + h:16 * b + h + 1],
                                                                 op0=ALU.is_equal, op1=ALU.mult),
                         reads=["bidx", "rbrep", "rbrep2"], writes=["btmp"])
                    S.op("pool", lambda: nc.gpsimd.tensor_tensor(out=ba[:], in0=ba[:], in1=btmp[:], op=ALU.add),
                         reads=["btmp", bk], writes=[bk])
                S.dma("sp", BT[h], ba[:], reads=[bk], writes=[("BT", h)], chan=("btst", hi % 2))
            S.barrier()

    bank_rr = [0]

    def next_bank():
        k = bank_rr[0]
        bank_rr[0] = (k + 1) % 8
        return pb[k], pbk[k]

    def rstd_from_ssq(ap_io, n_part, inv_n, rkeys, wkeys):
        S.op("act", lambda: nc.scalar.activation(out=ap_io, in_=ap_io, func=AF.Ln, scale=inv_n, bias=epsv[:n_part, :]),
             reads=list(rkeys) + ["epsv"], writes=wkeys)
        S.op("act", lambda: nc.scalar.activation(out=ap_io, in_=ap_io, func=AF.Exp, scale=-0.5),
             reads=wkeys, writes=wkeys)

    def make_hT(rows_ap, Tn, gvec, gkey, hT, hkey, xt_list, extra_read=()):
        for bi, (off, n) in enumerate(blocks_of(Tn)):
            xs, xkey = xt_list[bi % len(xt_list)]
            S.dma("sp", xs[:n, :], rows_ap[off:off + n, :], reads=list(extra_read), writes=[xkey])
            sq = ssq[:n, bi % 8:bi % 8 + 1]
            rr = rs[:n, bi % 8:bi % 8 + 1]
            sk = ("ssq", bi % 8)
            rk = ("rs", bi % 8)
            S.op("dve", lambda: nc.vector.scalar_tensor_tensor(out=junk[:n, :], in0=xs[:n, :], scalar=1.0, in1=xs[:n, :],
                                                                op0=ALU.mult, op1=ALU.mult, accum_out=sq),
                 reads=[xkey], writes=["junk", sk])
            S.op("act", lambda: nc.scalar.activation(out=rr, in_=sq, func=AF.Ln, scale=1.0 / D, bias=epsv[:n, :]),
                 reads=[sk, "epsv"], writes=[rk])
            S.op("act", lambda: nc.scalar.activation(out=rr, in_=rr, func=AF.Exp, scale=-0.5), reads=[rk], writes=[rk])
            S.op("dve", lambda: nc.vector.tensor_scalar(out=xs[:n, :], in0=xs[:n, :], scalar1=rr, scalar2=None, op0=ALU.mult),
                 reads=[xkey, rk], writes=[xkey])
            for c4 in range(8):
                bank, bkey = next_bank()
                for k in range(4):
                    c = c4 * 4 + k
                    S.op("pe", lambda: nc.tensor.transpose(out=bank[:, k * 128:k * 128 + n], in_=xs[:n, c * 128:(c + 1) * 128],
                                                           identity=ident[:n, :n]),
                         reads=[xkey, "ident"], writes=[bkey], sig=(k == 3))
                S.op("dve", lambda: nc.vector.tensor_tensor(
                    out=hT[:, c4 * 4:c4 * 4 + 4, off:off + n],
                    in0=bank[:, 0:512].rearrange("p (k t) -> p k t", k=4)[:, :, :n],
                    in1=gvec[:, c4 * 4:c4 * 4 + 4].unsqueeze(2).broadcast_to([128, 4, n]), op=ALU.mult),
                    reads=[bkey, gkey], writes=[hkey])

    wslot_rr = [0]

    def load_w(wslots, key, kc, width):
        s = wslot_rr[0] % len(wslots)
        wslot_rr[0] += 1
        wt, wk = wslots[s]
        view = wt[:, 0:kc * width].rearrange("p (c m) -> p c m", c=kc)
        S.dma("sp", view, wsc[key], reads=[("ws", key)], writes=[wk])
        return view, wk

    def f_proj(wv, wk, kc, m0, mw, rhsT, rkey, Tn, p0=0):
        bank, bkey = next_bank()
        for c in range(kc):
            S.op("pe", lambda: nc.tensor.matmul(bank[:mw, :Tn], lhsT=wv[:, c, m0:m0 + mw], rhs=rhsT[:, c, :Tn],
                                                start=(c == 0), stop=(c == kc - 1)),
                 reads=[wk, rkey], writes=[bkey], sig=(c == kc - 1))
        return bank, bkey

    def t_proj(wv, wk, kc, n0, nw, lhsT, lkey, off, n):
        bank, bkey = next_bank()
        for c in range(kc):
            S.op("pe", lambda: nc.tensor.matmul(bank[:n, :nw], lhsT=lhsT[:, c, off:off + n], rhs=wv[:, c, n0:n0 + nw],
                                                start=(c == 0), stop=(c == kc - 1)),
                 reads=[wk, lkey], writes=[bkey], sig=(c == kc - 1))
        return bank, bkey

    stg_rr = [0]

    def stage_store(stgs, bank, bkey, np_, ncol, dst, dkey, act_func=None):
        s = stg_rr[0] % len(stgs)
        stg_rr[0] += 1
        st, sk = stgs[s]
        if act_func is None:
            S.op("dve", lambda: nc.vector.tensor_copy(out=st[:np_, :ncol], in_=bank[:np_, :ncol]), reads=[bkey], writes=[sk])
        else:
            S.op("act", lambda: nc.scalar.activation(out=st[:np_, :ncol], in_=bank[:np_, :ncol], func=act_func), reads=[bkey], writes=[sk])
        S.dma("sp", dst, st[:np_, :ncol], reads=[sk], writes=[dkey], chan=("stg", s))

    def key_range(i):
        a = max(T * i, 1)
        b = min(T * i + T, 4097)
        return a - T * i, b - T * i, a - 1

    if "A" in phases:
        with ExitStack() as stA:
            xts = [(S.sb("xtA%d" % k, [128, D], F32, stA), ("xt", k)) for k in range(1)]
            junk = S.sb("junkA", [128, D], BF16, stA)
            hTs = [(S.sb("hTA%d" % k, [128, 32, 464], BF16, stA), ("hT", k)) for k in range(1)]
            wslots = [(S.sb("wsA%d" % k, [128, 32 * 512], BF16, stA), ("wslot", k)) for k in range(2)]
            stgs = [(S.sb("stgA%d" % k, [128, 512], BF16, stA), ("stgb", k)) for k in range(4)]
            qlat = S.sb("qlat", [128, 8, 464], F32, stA)
            kvlat = S.sb("kvlat", [128, 4, 464], F32, stA)
            sqt = S.sb("sqt", [128, 464], F32, stA)
            rrep = S.sb("rrep", [128, 464], F32, stA)
            qn = S.sb("qn", [128, 8, 464], BF16, stA)
            kvn = S.sb("kvn", [128, 4, 464], BF16, stA)
            pe_as = [(S.sb("pe_a%d" % k, [64, 464], F32, stA), ("pe_a", k)) for k in range(2)]
            pe_ss = [(S.sb("pe_s%d" % k, [64, 464], F32, stA), ("pe_s", k)) for k in range(2)]
            cs1 = S.sb("cs1", [64, 464], F32, stA)
            cs2 = S.sb("cs2", [64, 464], F32, stA)
            rope_rr = [0]

            def rope_store(bank, bkey, Tn, dst_fn):
                r = rope_rr[0] % 2
                rope_rr[0] += 1
                pa, pak = pe_as[r]
                ps_, psk = pe_ss[r]
                S.op("dve", lambda: nc.vector.tensor_copy(out=pa[:, :Tn], in_=bank[:64, :Tn]), reads=[bkey], writes=[pak])
                S.dma("sp", ps_[0:32, :Tn], pa[32:64, :Tn], reads=[pak], writes=[(psk, 0)], chan=("ropesw", r, 0))
                S.dma("sp", ps_[32:64, :Tn], pa[0:32, :Tn], reads=[pak], writes=[(psk, 1)], chan=("ropesw", r, 1))
                S.op("dve", lambda: nc.vector.tensor_tensor(out=pa[:, :Tn], in0=pa[:, :Tn], in1=cs1[:, :Tn], op=ALU.mult),
                     reads=[pak, "cs1"], writes=[pak])
                S.op("dve", lambda: nc.vector.tensor_tensor(out=ps_[:, :Tn], in0=ps_[:, :Tn], in1=cs2[:, :Tn], op=ALU.mult),
                     reads=[(psk, 0), (psk, 1), "cs2"], writes=[(psk, 0), (psk, 1)])
                s_ = stg_rr[0] % len(stgs); stg_rr[0] += 1
                st, sk = stgs[s_]
                S.op("dve", lambda: nc.vector.tensor_tensor(out=st[:64, :Tn], in0=pa[:, :Tn], in1=ps_[:, :Tn], op=ALU.add),
                     reads=[pak, (psk, 0), (psk, 1)], writes=[sk])
                dst_fn(st, sk, s_)

            tiles = [("own", i) for i in range(n_own_tiles)] + [("oth", i) for i in range(n_oth_tiles)]
            for ti, (kind, i) in enumerate(tiles):
                own = kind == "own"
                t0 = i * T
                Tn = T if own else min(T, NOTH - t0)
                rows = (xw if own else xo)[t0:t0 + Tn, :]
                hT, hkey = hTs[0]
                make_hT(rows, Tn, g_attn, "g_attn", hT, hkey, xts)
                if own:
                    c0, c1, k0 = key_range(i)
                else:
                    c0, c1, k0 = 0, Tn, 4096 + t0
                nkeys = c1 - c0
                S.dma("sp", cs1[:, :Tn], (cs1w if own else cs1o)[:, t0:t0 + Tn], writes=["cs1"])
                S.dma("sp", cs2[:, :Tn], (cs2w if own else cs2o)[:, t0:t0 + Tn], writes=["cs2"])
                for name, dst in ((("dq", QT),) if own else ()) + (("dk", KT),):
                    for j in range(4):
                        wv, wk = load_w(wslots, (name, j), 32, 512)
                        for m in range(4):
                            h = 4 * j + m
                            bank, bkey = f_proj(wv, wk, 32, 128 * m, 128, hT, hkey, Tn)
                            if name == "dq":
                                stage_store(stgs, bank, bkey, 128, Tn, dst[h, :, t0:t0 + Tn], ("QT", h))
                            elif nkeys > 0:
                                bb = bank[:, c0:c1]
                                s = stg_rr[0] % len(stgs); stg_rr[0] += 1
                                st, sk = stgs[s]
                                S.op("dve", lambda: nc.vector.tensor_copy(out=st[:, :nkeys], in_=bb), reads=[bkey], writes=[sk])
                                S.dma("sp", dst[h, :, k0:k0 + nkeys], st[:, :nkeys], reads=[sk], writes=[("KT", h)], chan=("stg", s))
                for j in range(4):
                    wv, wk = load_w(wslots, ("dv", j), 32, 512)
                    for (off, n) in blocks_of(Tn):
                        bank, bkey = t_proj(wv, wk, 32, 0, 512, hT, hkey, off, n)
                        a = max(off, c0); b = min(off + n, c1)
                        if b > a:
                            s = stg_rr[0] % len(stgs); stg_rr[0] += 1
                            st, sk = stgs[s]
                            S.op("dve", lambda: nc.vector.tensor_copy(out=st[:n, :], in_=bank[:n, :]), reads=[bkey], writes=[sk])
                            S.dma("sp", VV[k0 + a - c0:k0 + b - c0, 512 * j:512 * j + 512], st[a - off:b - off, :],
                                  reads=[sk], writes=[("VV", j)], chan=("stg", s))
                lat_list = ([("ql", 2, qlat, "qlat")] if own else []) + [("kvl", 1, kvlat, "kvlat")]
                for name, nb, lat, lkey in lat_list:
                    for j in range(nb):
                        wv, wk = load_w(wslots, (name, j), 32, 512)
                        for m in range(4):
                            bank, bkey = f_proj(wv, wk, 32, 128 * m, 128, hT, hkey, Tn)
                            S.op("dve", lambda: nc.vector.tensor_copy(out=lat[:, 4 * j + m, :Tn], in_=bank[:, :Tn]), reads=[bkey], writes=[lkey])
                wv, wk = load_w(wslots, ("kpe", 0), 32, 64)
                bank, bkey = f_proj(wv, wk, 32, 0, 64, hT, hkey, Tn)

                def kpe_dst(st, sk, s_):
                    if nkeys > 0:
                        S.dma("sp", KPE[:, k0:k0 + nkeys], st[:64, c0:c1], reads=[sk], writes=["KPE"], chan=("stg", s_))
                rope_store(bank, bkey, Tn, kpe_dst)
                if own:
                    for j in range(16):
                        wv, wk = load_w(wslots, ("gate", j), 32, 512)
                        for m in range(4):
                            bank, bkey = f_proj(wv, wk, 32, 128 * m, 128, hT, hkey, Tn)
                            stage_store(stgs, bank, bkey, 128, Tn, SG[4 * j + m, :, t0:t0 + Tn], ("SG", 4 * j + m), act_func=AF.Sigmoid)
                for name, nch, lat, lkey, gv, gk, outn, okey in (([("q", 8, qlat, "qlat", g_q, "g_q", qn, "qn")] if own else [])
                                                                 + [("kv", 4, kvlat, "kvlat", g_kv, "g_kv", kvn, "kvn")]):
                    bank, bkey = next_bank()
                    for c in range(nch):
                        S.op("dve", lambda: nc.vector.tensor_tensor(out=sqt[:, :Tn], in0=lat[:, c, :Tn], in1=lat[:, c, :Tn], op=ALU.mult),
                             reads=[lkey], writes=["sqt"])
                        S.op("pe", lambda: nc.tensor.matmul(bank[:, :Tn], lhsT=ones_f[:], rhs=sqt[:, :Tn], start=(c == 0), stop=(c == nch - 1)),
                             reads=["ones_f", "sqt"], writes=[bkey], sig=True)
                    S.op("act", lambda: nc.scalar.activation(out=rrep[:, :Tn], in_=bank[:, :Tn], func=AF.Ln, scale=1.0 / (128 * nch), bias=epsv[:]),
                         reads=[bkey, "epsv"], writes=["rrep"])
                    S.op("act", lambda: nc.scalar.activation(out=rrep[:, :Tn], in_=rrep[:, :Tn], func=AF.Exp, scale=-0.5), reads=["rrep"], writes=["rrep"])
                    for c in range(nch):
                        S.op("dve", lambda: nc.vector.scalar_tensor_tensor(out=outn[:, c, :Tn], in0=lat[:, c, :Tn], scalar=gv[:, c:c + 1], in1=rrep[:, :Tn],
                                                                            op0=ALU.mult, op1=ALU.mult),
                             reads=[lkey, gk, "rrep"], writes=[okey])
                if own:
                    for j in range(8):
                        wv, wk = load_w(wslots, ("qup", j), 8, 384)
                        for m in range(2):
                            h = 2 * j + m
                            bank, bkey = f_proj(wv, wk, 8, 192 * m, 128, qn, "qn", Tn)
                            stage_store(stgs, bank, bkey, 128, Tn, QN[h, :, t0:t0 + Tn], ("QN", h))
                            bank, bkey = f_proj(wv, wk, 8, 192 * m + 128, 64, qn, "qn", Tn)

                            def qpe_dst(st, sk, s_, h=h):
                                S.dma("sp", QPE[:, h, t0:t0 + Tn], st[:64, :Tn], reads=[sk], writes=[("QPE", h)], chan=("stg", s_))
                            rope_store(bank, bkey, Tn, qpe_dst)
                for j in range(8):
                    wv, wk = load_w(wslots, ("kvup", j), 4, 512)
                    for m in range(2):
                        h = 2 * j + m
                        bank, bkey = f_proj(wv, wk, 4, 256 * m, 128, kvn, "kvn", Tn)
                        if nkeys > 0:
                            bb = bank[:, c0:c1]
                            s = stg_rr[0] % len(stgs); stg_rr[0] += 1
                            st, sk = stgs[s]
                            S.op("dve", lambda: nc.vector.tensor_copy(out=st[:, :nkeys], in_=bb), reads=[bkey], writes=[sk])
                            S.dma("sp", KN[h, :, k0:k0 + nkeys], st[:, :nkeys], reads=[sk], writes=[("KN", h)], chan=("stg", s))
                        for (off, n) in blocks_of(Tn):
                            bank, bkey = t_proj(wv, wk, 4, 256 * m + 128, 128, kvn, "kvn", off, n)
                            a = max(off, c0); b = min(off + n, c1)
                            if b > a:
                                s = stg_rr[0] % len(stgs); stg_rr[0] += 1
                                st, sk = stgs[s]
                                S.op("dve", lambda: nc.vector.tensor_copy(out=st[:n, :128], in_=bank[:n, :128]), reads=[bkey], writes=[sk])
                                S.dma("sp", VM[k0 + a - c0:k0 + b - c0, 128 * h:128 * h + 128], st[a - off:b - off, :128],
                                      reads=[sk], writes=[("VM", h)], chan=("stg", s))
            S.barrier()

    if "B" in phases:
        with ExitStack() as stB:
            kt_sb = S.sb("kt_sb", [128, NK], BF16, stB)
            v_sb = S.sb("v_sb", [128, 64, 128], BF16, stB)
            kn_sb = S.sb("kn_sb", [128, NK], BF16, stB)
            vm_sb = S.sb("vm_sb", [128, 64, 128], BF16, stB)
            kpe_sb = S.sb("kpe_sb", [64, NK], BF16, stB)
            bts = [(S.sb("bt%d" % k, [128, WB], F32, stB), ("bt", k)) for k in range(2)]
            qts = [(S.sb("qt%d" % k, [128, T], BF16, stB), ("qt", k)) for k in range(2)]
            qns = [(S.sb("qnb%d" % k, [128, T], BF16, stB), ("qnb", k)) for k in range(2)]
            qps = [(S.sb("qpb%d" % k, [64, T], BF16, stB), ("qpb", k)) for k in range(2)]
            NP = 6
            pts = [(S.sb("pt%d" % k, [128, T], BF16, stB), ("pt", k)) for k in range(NP)]
            tmps = [(S.sb("tmpb%d" % k, [128, T], F32, stB), ("tmpb", k)) for k in range(2)]
            fa = S.sb("fa", [128, T], F32, stB)
            fb = S.sb("fb", [128, T], F32, stB)
            fc = S.sb("fc", [128, T], F32, stB)
            ost = [(S.sb("ost%d" % k, [128, T], BF16, stB), ("ost", k)) for k in range(2)]
            S.dma("sp", kpe_sb[:], KPE, writes=["kpe_sb"])
            prr = [0]
            trr = [0]
            orr = [0]
            qrr = [0]
            ktiles = list(range(nkt_own)) + [32 + jj for jj in range(nkt_oth)]

            def exp_tile(sbank, sbkey, scale, mode, bias_ap, bias_keys):
                pt, pk = pts[prr[0] % NP]
                prr[0] += 1
                if mode == "const":
                    S.op("act", lambda: nc.scalar.activation(out=pt[:], in_=sbank[:, :T], func=AF.Exp, scale=scale, bias=bias_ap),
                         reads=[sbkey] + bias_keys, writes=[pk])
                else:
                    tm, tk = tmps[trr[0] % 2]
                    trr[0] += 1
                    S.op("dve", lambda: nc.vector.scalar_tensor_tensor(out=tm[:], in0=sbank[:, :T], scalar=scale, in1=bias_ap,
                                                                        op0=ALU.mult, op1=ALU.add),
                         reads=[sbkey] + bias_keys, writes=[tk])
                    S.op("act", lambda: nc.scalar.activation(out=pt[:], in_=tm[:], func=AF.Exp), reads=[tk], writes=[pk])
                return pt, pk

            for hi, h in enumerate(heads):
                bt, btk = bts[hi % 2]
                S.dma("sp", bt[:], BT[h], writes=[btk])
                S.dma("sp", kt_sb[:], KT[h], writes=["kt_sb"])
                for q4 in range(4):
                    S.dma("sp", v_sb[:, 16 * q4:16 * q4 + 16, :],
                          VV[2048 * q4:2048 * q4 + 2048, 128 * h:128 * h + 128].rearrange("(j p) e -> p j e", p=128),
                          writes=[("v_sb", q4)], chan=("v_sb", q4))
                vkeys = [("v_sb", q4) for q4 in range(4)]
                cplus = rbrep[:, 16 * 31 + h:16 * 31 + h + 1]
                cminus = rbrep[:, 16 * 15 + h:16 * 15 + h + 1]
                cother = coth[:, h:h + 1]
                for i in range(n_own_tiles):
                    qt, qk = qts[qrr[0] % 2]
                    S.dma("sp", qt[:], QT[h, :, T * i:T * i + T], writes=[qk])
                    O0, O0k = pb[4], pbk[4]
                    O1, O1k = pb[5], pbk[5]
                    L0, L0k = pb[6], pbk[6]
                    L1, L1k = pb[7], pbk[7]
                    for ji, j in enumerate(ktiles):
                        sb0, sk0 = pb[(2 * ji) % 4], pbk[(2 * ji) % 4]
                        sb1, sk1 = pb[(2 * ji) % 4 + 1], pbk[(2 * ji) % 4 + 1]
                        S.op("pe", lambda: nc.tensor.matmul(sb0[:, :T], lhsT=kt_sb[0:64, 128 * j:128 * j + 128], rhs=qt[0:64, :], start=True, stop=True),
                             reads=["kt_sb", qk], writes=[sk0])
                        S.op("pe", lambda: nc.tensor.matmul(sb1[:, :T], lhsT=kt_sb[64:128, 128 * j:128 * j + 128], rhs=qt[64:128, :], start=True, stop=True),
                             reads=["kt_sb", qk], writes=[sk1])
                        if j < 32:
                            dl = 128 * j - T * i + 1
                            if dl > U0:
                                mode, bap, bkeys = "const", cplus, ["rbrep"]
                            elif dl < -217:
                                mode, bap, bkeys = "const", cminus, ["rbrep"]
                            else:
                                u0 = U0 - dl
                                mode, bap, bkeys = "tile", bt[:, u0:u0 + T], [btk]
                        elif i == 8 and j == 32:
                            mode, bap, bkeys = "tile", bt[:, WT:WT + T], [btk]
                        elif i == 0 and j == 63:
                            mode, bap, bkeys = "tile", bt[:, WT + T:WT + 2 * T], [btk]
                        else:
                            mode, bap, bkeys = "const", cother, ["coth"]
                        p0, pk0 = exp_tile(sb0, sk0, SC_DIFF, mode, bap, bkeys)
                        p1, pk1 = exp_tile(sb1, sk1, SC_DIFF, mode, bap, bkeys)
                        first = ji == 0
                        last = ji == len(ktiles) - 1
                        vk = [vkeys[j // 16]]
                        S.op("pe", lambda: nc.tensor.matmul(O0[:, :T], lhsT=v_sb[:, j, :], rhs=p0[:], start=first, stop=last), reads=vk + [pk0], writes=[O0k], sig=True)
                        S.op("pe", lambda: nc.tensor.matmul(L0[:, :T], lhsT=ones_b[:], rhs=p0[:], start=first, stop=last), reads=["ones_b", pk0], writes=[L0k], sig=True)
                        S.op("pe", lambda: nc.tensor.matmul(O1[:, :T], lhsT=v_sb[:, j, :], rhs=p1[:], start=first, stop=last), reads=vk + [pk1], writes=[O1k], sig=True)
                        S.op("pe", lambda: nc.tensor.matmul(L1[:, :T], lhsT=ones_b[:], rhs=p1[:], start=first, stop=last), reads=["ones_b", pk1], writes=[L1k], sig=True)
                    S.op("dve", lambda: nc.vector.reciprocal(out=fa[:], in_=L0[:, :T]), reads=[L0k], writes=["fa"])
                    S.op("dve", lambda: nc.vector.tensor_tensor(out=fa[:], in0=O0[:, :T], in1=fa[:], op=ALU.mult), reads=[O0k, "fa"], writes=["fa"])
                    S.op("dve", lambda: nc.vector.reciprocal(out=fb[:], in_=L1[:, :T]), reads=[L1k], writes=["fb"])
                    S.op("dve", lambda: nc.vector.tensor_tensor(out=fb[:], in0=O1[:, :T], in1=fb[:], op=ALU.mult), reads=[O1k, "fb"], writes=["fb"])
                    S.op("dve", lambda: nc.vector.scalar_tensor_tensor(out=fa[:], in0=fb[:], scalar=neglam[:], in1=fa[:], op0=ALU.mult, op1=ALU.add),
                         reads=["fa", "fb", "neglam"], writes=["fa"])
                    S.op("dve", lambda: nc.vector.tensor_tensor(out=fb[:], in0=fa[:], in1=fa[:], op=ALU.mult), reads=["fa"], writes=["fb"])
                    S.op("pe", lambda: nc.tensor.matmul(pb[0][:, :T], lhsT=ones_f[:], rhs=fb[:], start=True, stop=True), reads=["ones_f", "fb"], writes=[pbk[0]])
                    S.op("act", lambda: nc.scalar.activation(out=fc[:], in_=pb[0][:, :T], func=AF.Ln, scale=1.0 / 128, bias=epsv[:]),
                         reads=[pbk[0], "epsv"], writes=["fc"])
                    S.op("act", lambda: nc.scalar.activation(out=fc[:], in_=fc[:], func=AF.Exp, scale=-0.5), reads=["fc"], writes=["fc"])
                    os_, osk = ost[orr[0] % 2]
                    orr[0] += 1
                    S.op("dve", lambda: nc.vector.scalar_tensor_tensor(out=os_[:], in0=fa[:], scalar=gsub[:], in1=fc[:], op0=ALU.mult, op1=ALU.mult),
                         reads=["fa", "fc", "gsub"], writes=[osk])
                    S.dma("sp", AO[h, :, T * i:T * i + T], os_[:], reads=[osk], writes=[("AO", h)], chan=("ost", (orr[0] - 1) % 2))
                    qrr[0] += 1
                S.dma("sp", kn_sb[:], KN[h], writes=["kn_sb"])
                for q4 in range(4):
                    S.dma("sp", vm_sb[:, 16 * q4:16 * q4 + 16, :],
                          VM[2048 * q4:2048 * q4 + 2048, 128 * h:128 * h + 128].rearrange("(j p) e -> p j e", p=128),
                          writes=[("vm_sb", q4)], chan=("vm_sb", q4))
                vmkeys = [("vm_sb", q4) for q4 in range(4)]
                for i in range(n_own_tiles):
                    qn_, qnk = qns[qrr[0] % 2]
                    qp_, qpk = qps[qrr[0] % 2]
                    S.dma("sp", qn_[:], QN[h, :, T * i:T * i + T], writes=[qnk])
                    S.dma("sp", qp_[:], QPE[:, h, T * i:T * i + T], writes=[qpk])
                    O, Ok = pb[4], pbk[4]
                    L, Lk = pb[5], pbk[5]
                    for ji, j in enumerate(ktiles):
                        sbk_i = ji % 4
                        sbn, skn = pb[sbk_i], pbk[sbk_i]
                        S.op("pe", lambda: nc.tensor.matmul(sbn[:, :T], lhsT=kn_sb[:, 128 * j:128 * j + 128], rhs=qn_[:], start=True, stop=False),
                             reads=["kn_sb", qnk], writes=[skn], sig=False)
                        S.op("pe", lambda: nc.tensor.matmul(sbn[:, :T], lhsT=kpe_sb[:, 128 * j:128 * j + 128], rhs=qp_[:], start=False, stop=True),
                             reads=["kpe_sb", qpk], writes=[skn])
                        if j < 32:
                            p, pk = exp_tile(sbn, skn, SC_MLA, "const", 0.0, [])
                        else:
                            p, pk = exp_tile(sbn, skn, SC_MLA, "const", cvals[:, 33:34], ["cvals"])
                        first = ji == 0
                        last = ji == len(ktiles) - 1
                        S.op("pe", lambda: nc.tensor.matmul(O[:, :T], lhsT=vm_sb[:, j, :], rhs=p[:], start=first, stop=last), reads=[vmkeys[j // 16], pk], writes=[Ok], sig=True)
                        S.op("pe", lambda: nc.tensor.matmul(L[:, :T], lhsT=ones_b[:], rhs=p[:], start=first, stop=last), reads=["ones_b", pk], writes=[Lk], sig=True)
                    S.op("dve", lambda: nc.vector.reciprocal(out=fa[:], in_=L[:, :T]), reads=[Lk], writes=["fa"])
                    os_, osk = ost[orr[0] % 2]
                    orr[0] += 1
                    S.op("dve", lambda: nc.vector.tensor_tensor(out=os_[:], in0=O[:, :T], in1=fa[:], op=ALU.mult), reads=[Ok, "fa"], writes=[osk])
                    S.dma("sp", BO[h, :, T * i:T * i + T], os_[:], reads=[osk], writes=[("BO", h)], chan=("ost", (orr[0] - 1) % 2))
                    qrr[0] += 1
            S.barrier()

    if "C" in phases:
        with ExitStack() as stC:
            a_sb = S.sb("a_sb", [128, 16, T], BF16, stC)
            b_sb = S.sb("b_sb", [128, 16, T], BF16, stC)
            sgs = [(S.sb("sg%d" % k, [128, 2, T], BF16, stC), ("sg", k)) for k in range(2)]
            mT = S.sb("mT", [128, 32, T], BF16, stC)
            wslots = [(S.sb("wsC%d" % k, [128, 32 * 512], BF16, stC), ("wslot", k)) for k in range(2)]
            t1 = S.sb("t1c", [128, T], F32, stC)
            t2 = S.sb("t2c", [128, T], F32, stC)
            xbs = [(S.sb("xb%d" % k, [128, 512], F32, stC), ("xb", k)) for k in range(3)]
            xrr = [0]
            for i in range(n_own_tiles):
                t0 = T * i
                for hh in range(0, 16, 4):
                    S.dma("sp", a_sb[:, hh:hh + 4, :], AO[hh:hh + 4, :, t0:t0 + T].rearrange("h p t -> p h t"), writes=[("a_sb", hh)], chan=("a_sb", hh))
                    S.dma("sp", b_sb[:, hh:hh + 4, :], BO[hh:hh + 4, :, t0:t0 + T].rearrange("h p t -> p h t"), writes=[("b_sb", hh)], chan=("b_sb", hh))
                akeys = [("a_sb", hh) for hh in range(0, 16, 4)]
                bkeys_ = [("b_sb", hh) for hh in range(0, 16, 4)]
                for j in range(8):
                    wva, wka = load_w(wslots[0:1], ("wa", j), 16, 512)
                    wvb, wkb = load_w(wslots[1:2], ("wb", j), 16, 512)
                    for m in range(4):
                        mm_ = 4 * j + m
                        sg, sgk = sgs[mm_ % 2]
                        S.dma("sp", sg[:, 0, :], SG[mm_, :, t0:t0 + T], writes=[sgk], chan=("sgl", mm_ % 2, 0))
                        S.dma("sp", sg[:, 1, :], SG[32 + mm_, :, t0:t0 + T], writes=[(sgk, 1)], chan=("sgl", mm_ % 2, 1))
                        bankA, bkA = next_bank()
                        for hh in range(16):
                            S.op("pe", lambda: nc.tensor.matmul(bankA[:, :T], lhsT=wva[:, hh, 128 * m:128 * m + 128], rhs=a_sb[:, hh, :], start=(hh == 0), stop=(hh == 15)),
                                 reads=[wka] + akeys, writes=[bkA], sig=(hh == 15))
                        bankB, bkB = next_bank()
                        for hh in range(16):
                            S.op("pe", lambda: nc.tensor.matmul(bankB[:, :T], lhsT=wvb[:, hh, 128 * m:128 * m + 128], rhs=b_sb[:, hh, :], start=(hh == 0), stop=(hh == 15)),
                                 reads=[wkb] + bkeys_, writes=[bkB], sig=(hh == 15))
                        S.op("dve", lambda: nc.vector.tensor_tensor(out=t1[:], in0=bankA[:, :T], in1=sg[:, 0, :], op=ALU.mult), reads=[bkA, sgk], writes=["t1c"])
                        S.op("dve", lambda: nc.vector.tensor_tensor(out=t2[:], in0=bankB[:, :T], in1=sg[:, 1, :], op=ALU.mult), reads=[bkB, (sgk, 1)], writes=["t2c"])
                        S.op("dve", lambda: nc.vector.tensor_tensor(out=mT[:, mm_, :], in0=t1[:], in1=t2[:], op=ALU.add), reads=["t1c", "t2c"], writes=["mT"])
                for n in range(8):
                    wv, wk = load_w(wslots, ("wo", n), 32, 512)
                    for (off, nn) in blocks_of(T):
                        xb, xbk = xbs[xrr[0] % 3]
                        xrr[0] += 1
                        S.dma("sp", xb[:nn, :], xw[t0 + off:t0 + off + nn, 512 * n:512 * n + 512], writes=[xbk])
                        bank, bkey = t_proj(wv, wk, 32, 0, 512, mT, "mT", off, nn)
                        S.op("dve", lambda: nc.vector.tensor_tensor(out=xb[:nn, :], in0=bank[:nn, :], in1=xb[:nn, :], op=ALU.add), reads=[bkey, xbk], writes=[xbk])
                        S.dma("sp", X1[t0 + off:t0 + off + nn, 512 * n:512 * n + 512], xb[:nn, :], reads=[xbk], writes=["X1"], chan=("xbst", (xrr[0] - 1) % 3))
            S.barrier()

    if "D" in phases:
        with ExitStack() as stD:
            xts = [(S.sb("xtD", [128, D], F32, stD), ("xt", 0))]
            junk = S.sb("junkD", [128, D], BF16, stD)
            grep = S.sb("grep", [128, D], F32, stD)
            h2T = S.sb("h2T", [128, 32, 464], BF16, stD)
            actT = S.sb("actT", [128, 43, 464], BF16, stD)
            wslots = [(S.sb("wsD%d" % k, [128, 32 * 256], BF16, stD), ("wslot", k)) for k in range(3)]
            dslots = [(S.sb("wdD%d" % k, [128, 8 * 512], BF16, stD), ("dslot", k)) for k in range(2)]
            cg_ = S.sb("cg_", [128, 464], F32, stD)
            cu_ = S.sb("cu_", [128, 464], F32, stD)
            cs_ = S.sb("cs_", [128, 464], F32, stD)
            xbs = [(S.sb("xbD%d" % k, [128, 512], F32, stD), ("xb", k)) for k in range(3)]
            ssqp = S.sb("ssqp", [128, 4, 8], F32, stD)
            rsf = S.sb("rsf", [128, 4], F32, stD)
            xrr = [0]
            S.dma("sp", grep[:], final_norm_g.partition_broadcast(128), writes=["grep"])
            S.op("dve", lambda: nc.vector.memset(actT[:], 0.0), writes=["actT"])
            for i in range(n_own_tiles):
                r0 = T * i
                Tc = min(T + 2, WIN - r0)
                make_hT(X1[r0:r0 + Tc, :], Tc, g_ffn, "g_ffn", h2T, "h2T", xts, extra_read=["X1"])
                if i == 0:
                    S.op("dve", lambda: nc.vector.tensor_scalar(out=h2T[:, :, 0:1], in0=h2T[:, :, 0:1], scalar1=cvals[:, 34:35], scalar2=None, op0=ALU.mult),
                         reads=["h2T", "cvals"], writes=["h2T"])
                if i == 8:
                    S.op("dve", lambda: nc.vector.tensor_scalar(out=h2T[:, :, 449:450], in0=h2T[:, :, 449:450], scalar1=cvals[:, 35:36], scalar2=None, op0=ALU.mult),
                         reads=["h2T", "cvals"], writes=["h2T"])
                W2 = Tc - 2
                for g in range(2):
                    for fl in range(43):
                        f = 43 * g + fl
                        wv, wk = load_w(wslots, ("up", f), 32, 256)
                        bankG, bkG = f_proj(wv, wk, 32, 0, 128, h2T, "h2T", Tc)
                        bankU, bkU = f_proj(wv, wk, 32, 128, 128, h2T, "h2T", Tc)
                        for (bank, bkey, dst, dk, cidx) in ((bankG, bkG, cg_, "cg_", f), (bankU, bkU, cu_, "cu_", 86 + f)):
                            S.op("dve", lambda: nc.vector.tensor_scalar(out=dst[:, 1:1 + W2], in0=bank[:, 1:1 + W2], scalar1=cw[:, 1, cidx:cidx + 1], scalar2=cb[:, cidx:cidx + 1],
                                                                        op0=ALU.mult, op1=ALU.add),
                                 reads=[bkey, ("cw", 1), "cb"], writes=[dk])
                            S.op("dve", lambda: nc.vector.scalar_tensor_tensor(out=dst[:, 1:1 + W2], in0=bank[:, 0:W2], scalar=cw[:, 0, cidx:cidx + 1], in1=dst[:, 1:1 + W2],
                                                                                op0=ALU.mult, op1=ALU.add),
                                 reads=[bkey, ("cw", 0), dk], writes=[dk])
                            S.op("dve", lambda: nc.vector.scalar_tensor_tensor(out=dst[:, 1:1 + W2], in0=bank[:, 2:2 + W2], scalar=cw[:, 2, cidx:cidx + 1], in1=dst[:, 1:1 + W2],
                                                                                op0=ALU.mult, op1=ALU.add),
                                 reads=[bkey, ("cw", 2), dk], writes=[dk])
                        S.op("act", lambda: nc.scalar.activation(out=cs_[:, 1:1 + W2], in_=cg_[:, 1:1 + W2], func=AF.Silu), reads=["cg_"], writes=["cs_"])
                        S.op("dve", lambda: nc.vector.tensor_tensor(out=actT[:, fl, 1:1 + W2], in0=cs_[:, 1:1 + W2], in1=cu_[:, 1:1 + W2], op=ALU.mult),
                             reads=["cs_", "cu_"], writes=["actT"])
                    cgs = [cg for cg in range(NCG_DOWN) if (cg * 8) // 43 == g or (min(cg * 8 + 7, 85)) // 43 == g]
                    for n in range(8):
                        banks = [next_bank() for _ in blocks_of(Tc)]
                        mmlist = []
                        for cg in cgs:
                            kc = min(8, 86 - cg * 8)
                            for c in range(kc):
                                f = cg * 8 + c
                                if f // 43 == g:
                                    mmlist.append((cg, c, f))
                        cur = None
                        for idx, (cg, c, f) in enumerate(mmlist):
                            if cur is None or cur[0] != cg:
                                kc = min(8, 86 - cg * 8)
                                s = wslot_rr[0] % 2
                                wslot_rr[0] += 1
                                wt, wkk = dslots[s]
                                wvv = wt[:, 0:kc * 512].rearrange("p (c m) -> p c m", c=kc)
                                S.dma("sp", wvv, wsc[("down", n, cg)], reads=[("ws", ("down", n, cg))], writes=[wkk])
                                cur = (cg, wvv, wkk)
                            lastcg = (idx == len(mmlist) - 1) or (mmlist[idx + 1][0] != cg)
                            blks = blocks_of(Tc)
                            for bi, (off, nn) in enumerate(blks):
                                bank, bkey = banks[bi]
                                S.op("pe", lambda: nc.tensor.matmul(bank[:nn, :], lhsT=actT[:, f - 43 * g, off:off + nn], rhs=cur[1][:, c, :],
                                                                    start=(idx == 0), stop=(idx == len(mmlist) - 1)),
                                     reads=["actT", cur[2]], writes=[bkey], sig=(idx == len(mmlist) - 1) or (lastcg and bi == len(blks) - 1))
                        for bi, (off, nn) in enumerate(blocks_of(Tc)):
                            bank, bkey = banks[bi]
                            xb, xbk = xbs[xrr[0] % 3]
                            xrr[0] += 1
                            src = X1 if g == 0 else X2
                            rb0 = r0 if g == 0 else 464 * i
                            S.dma("sp", xb[:nn, :], src[rb0 + off:rb0 + off + nn, 512 * n:512 * n + 512], reads=["X1", ("X2", i, n, bi)], writes=[xbk])
                            S.op("dve", lambda: nc.vector.tensor_tensor(out=xb[:nn, :], in0=bank[:nn, :], in1=xb[:nn, :], op=ALU.add), reads=[bkey, xbk], writes=[xbk])
                            if g == 1:
                                S.op("dve", lambda: nc.vector.scalar_tensor_tensor(out=junk[:nn, 0:512], in0=xb[:nn, :], scalar=1.0, in1=xb[:nn, :], op0=ALU.mult, op1=ALU.mult,
                                                                                    accum_out=ssqp[:nn, bi, n:n + 1]),
                                     reads=[xbk], writes=["junk", ("ssqp", bi, n)])
                            S.dma("sp", X2[464 * i + off:464 * i + off + nn, 512 * n:512 * n + 512], xb[:nn, :], reads=[xbk], writes=[("X2", i, n, bi)],
                                  chan=("xbst", (xrr[0] - 1) % 3))
                for bi, (off, nn) in enumerate(blocks_of(Tc)):
                    xs, xkey = xts[0]
                    S.dma("sp", xs[:nn, :], X2[464 * i + off:464 * i + off + nn, :], reads=[("X2", i, n, bi) for n in range(8)], writes=[xkey])
                    S.op("dve", lambda: nc.vector.tensor_reduce(out=rsf[:nn, bi:bi + 1], in_=ssqp[:nn, bi, :], axis=mybir.AxisListType.X, op=ALU.add),
                         reads=[("ssqp", bi, n) for n in range(8)], writes=[("rsf", bi)])
                    S.op("act", lambda: nc.scalar.activation(out=rsf[:nn, bi:bi + 1], in_=rsf[:nn, bi:bi + 1], func=AF.Ln, scale=1.0 / D, bias=epsv[:nn, :]),
                         reads=[("rsf", bi), "epsv"], writes=[("rsf", bi)])
                    S.op("act", lambda: nc.scalar.activation(out=rsf[:nn, bi:bi + 1], in_=rsf[:nn, bi:bi + 1], func=AF.Exp, scale=-0.5),
                         reads=[("rsf", bi)], writes=[("rsf", bi)])
                    S.op("dve", lambda: nc.vector.scalar_tensor_tensor(out=xs[:nn, :], in0=xs[:nn, :], scalar=rsf[:nn, bi:bi + 1], in1=grep[:nn, :], op0=ALU.mult, op1=ALU.mult),
                         reads=[xkey, ("rsf", bi), "grep"], writes=[xkey])
                    a = max(off, 1)
                    b = min(off + nn, T + 1, 4097 - r0)
                    if b > a:
                        S.dma("sp", y_out[r0 + a - 1:r0 + b - 1, :], xs[a - off:b - off, :], reads=[xkey], writes=[("y", i, bi)], chan="yout")
            S.barrier()

    S.finish()
    return nc, S


def _t5_bucket(rel):
    rel = np.asarray(rel, np.int32)
    nb, me = 16, 8
    ret = (rel > 0).astype(np.int32) * nb
    n = np.abs(rel)
    nf = np.maximum(n, me).astype(np.float32)
    large = me + (np.log(nf / np.float32(me)) / np.float32(math.log(128 / 8)) * np.float32(nb - me)).astype(np.int32)
    large = np.minimum(large, nb - 1)
    return ret + np.where(n < me, n, large)


def _rope_tables(pos):
    inv_freq = (np.float32(10000.0) ** (-np.arange(0, 64, 2, dtype=np.float32) / np.float32(64))).astype(np.float32)
    ang = pos.astype(np.float32)[None, :] * inv_freq[:, None]
    c = np.cos(ang).astype(np.float32)
    s = np.sin(ang).astype(np.float32)
    return np.concatenate([c, c], 0), np.concatenate([-s, s], 0)


def core_inputs(core, x_prompt, x_sample):
    if core < 4:
        own = x_prompt[core]
        oth = np.zeros((NOTH, D), np.float32)
        Lrow = np.zeros(D, np.float32)
        Rrow = np.zeros(D, np.float32)
        own_start, oth_start, typ = 0, 4096, "C"
    else:
        s, half = (core - 4) // 2, (core - 4) % 2
        own = x_sample[s, 4096 * half:4096 * half + 4096]
        oth = x_sample[s, 4096 * (1 - half):4096 * (1 - half) + 4096]
        own_start, oth_start = 4096 * half, 4096 * (1 - half)
        if half == 0:
            Lrow = np.zeros(D, np.float32); Rrow = x_sample[s, 4096]; typ = "A"
        else:
            Lrow = x_sample[s, 4095]; Rrow = np.zeros(D, np.float32); typ = "B"
    xw = np.zeros((WIN, D), np.float32)
    xw[0] = Lrow
    xw[1:4097] = own
    xw[4097] = Rrow
    wpos = own_start - 1 + np.arange(WIN)
    opos = oth_start + np.arange(NOTH)
    cs1w, cs2w = _rope_tables(wpos)
    cs1o, cs2o = _rope_tables(opos)
    p = np.arange(128)[:, None]
    u = np.arange(WT)[None, :]
    bidx = np.zeros((128, WB), np.float32)
    bidx[:, :WT] = _t5_bucket(p - u + U0)
    tq = np.arange(T)[None, :]
    if typ == "A":
        bidx[:, WT:WT + T] = _t5_bucket((4096 + p) - (3647 + tq))
        bidx[:, WT + T:] = 31
        sel, cm, mL, mR = 31, 0.0, 0.0, 1.0
    elif typ == "B":
        bidx[:, WT:WT + T] = 15
        bidx[:, WT + T:] = _t5_bucket((3968 - 4096 + p) - (-1 + tq))
        sel, cm, mL, mR = 15, 0.0, 1.0, 0.0
    else:
        bidx[:, WT:] = 32
        sel, cm, mL, mR = 32, NEG, 0.0, 0.0
    cvals = np.zeros((128, 40), np.float32)
    cvals[:, sel] = 1.0
    cvals[:, 33] = cm
    cvals[:, 34] = mL
    cvals[:, 35] = mR
    return {"xw": xw, "xo": np.ascontiguousarray(oth), "cs1w": cs1w, "cs2w": cs2w, "cs1o": cs1o, "cs2o": cs2o,
            "bidx": bidx, "cvals": cvals, "ident": np.eye(128, dtype=np.float32)}


WEIGHT_NAMES = ["rel_bias", "final_norm_g", "rms_attn_g", "w_in", "lambda_q1", "lambda_k1", "lambda_q2", "lambda_k2",
                "diff_subln_g", "mla_q_norm_g", "w_mla_q_up", "mla_kv_norm_g", "w_mla_kv_up", "w_branch_a",
                "w_branch_b", "w_out", "rms_ffn_g", "w_ffn_up", "conv_w", "conv_b", "w_ffn_down"]


def kernel(**inputs):
    x_prompt = np.asarray(inputs["x_prompt"], np.float32)
    x_sample = np.asarray(inputs["x_sample"], np.float32)
    shared = {k: np.ascontiguousarray(np.asarray(inputs[k], np.float32)) for k in WEIGHT_NAMES}
    nc, _ = build_program()
    in_maps = []
    for core in range(8):
        m = dict(shared)
        m.update(core_inputs(core, x_prompt, x_sample))
        in_maps.append(m)
    res = run_bass_kernel_spmd(nc, in_maps, core_ids=list(range(8)))
    ys = [np.asarray(r["y"], np.float32) for r in res.results]
    y_prompt = np.stack(ys[0:4], 0)
    y_sample = np.stack([np.concatenate([ys[4], ys[5]], 0), np.concatenate([ys[6], ys[7]], 0)], 0)
    return (y_prompt, y_sample)
```

```python
import math
import numpy as np
import concourse.bass as bass
import concourse.mybir as mybir
from concourse.bass_utils import run_bass_kernel_spmd
from contextlib import ExitStack

F32 = mybir.dt.float32
BF16 = mybir.dt.bfloat16
ALU = mybir.AluOpType
AF = mybir.ActivationFunctionType

D = 4096
H = 16
T = 456
NTI = 9
WIN = NTI * T
NOTH = 4096
NK = 8192
IN_COLS = 15936
DFF = 11008
EPS = 1e-6
NEG = -30000.0
U0 = 545
WT = 1218
WB = WT + 2 * T
SC_DIFF = 64 ** -0.5
SC_MLA = 192 ** -0.5


class Sched:
    def __init__(self, nc):
        self.nc = nc
        self.E = {"pe": nc.tensor, "act": nc.scalar, "dve": nc.vector,
                  "pool": nc.gpsimd, "sp": nc.sync}
        self.sem = {e: nc.alloc_semaphore(name="prog_" + e) for e in self.E}
        self.cnt = {e: 0 for e in self.E}
        self.known = {e: {} for e in self.E}
        self.last_w = {}
        self.readers = {}
        self.chan = {}
        self.pe_pr = set()
        self.pe_pw = set()
        self.stack = ExitStack()
        self.nwait = 0
        self.nins = 0
        self._uid = 0

    def sb(self, name, shape, dt, stack=None):
        return (stack or self.stack).enter_context(self.nc.sbuf_tensor("s_" + name, list(shape), dt))

    def ps(self, name, shape, dt=F32):
        return self.stack.enter_context(self.nc.psum_tensor(name, list(shape), dt))

    def _semof(self, key):
        if isinstance(key, tuple):
            return self.chan[key[1]][0]
        return self.sem[key]

    def _deps(self, eng, reads, writes):
        toks = []
        for b in reads:
            if eng != "pe" and b in self.pe_pw:
                raise RuntimeError(f"buffer {b} has pending unsignalled PE write")
            t = self.last_w.get(b)
            if t is not None:
                toks.append(t)
        for b in writes:
            if eng != "pe" and (b in self.pe_pw or b in self.pe_pr):
                raise RuntimeError(f"buffer {b} has pending unsignalled PE access")
            t = self.last_w.get(b)
            if t is not None:
                toks.append(t)
            toks.extend(self.readers.get(b, ()))
        need = {}
        for key, val in toks:
            if key == "pe" and eng == "pe":
                continue
            if need.get(key, 0) < val:
                need[key] = val
        e = self.E[eng]
        kn = self.known[eng]
        for key, val in need.items():
            if kn.get(key, 0) < val:
                e.wait_ge(self._semof(key), val)
                kn[key] = val
                self.nwait += 1

    def _register(self, tok, reads, writes):
        for b in reads:
            self.readers.setdefault(b, []).append(tok)
        for b in writes:
            self.last_w[b] = tok
            self.readers[b] = []

    def op(self, eng, fn, reads=(), writes=(), sig=True):
        self._deps(eng, reads, writes)
        ins = fn()
        self.nins += 1
        if eng == "pe" and not sig:
            self.pe_pr.update(reads)
            self.pe_pw.update(writes)
            return ins
        self.cnt[eng] += 1
        ins.then_inc(self.sem[eng], 1)
        tok = (eng, self.cnt[eng])
        if eng == "pe":
            reads = set(reads) | self.pe_pr
            writes = set(writes) | self.pe_pw
            self.pe_pr = set()
            self.pe_pw = set()
        self._register(tok, reads, writes)
        return ins

    def dma(self, eng, out, in_, reads=(), writes=(), chan=None, **kw):
        if chan is None:
            chan = writes[0] if writes else reads[0]
        if chan not in self.chan:
            self.chan[chan] = [self.nc.alloc_semaphore(name="dch%d" % len(self.chan)), 0]
        self._deps(eng, reads, writes)
        c = self.chan[chan]
        ins = self.E[eng].dma_start(out=out, in_=in_, **kw)
        ins.then_inc(c[0], 16)
        c[1] += 16
        self.nins += 1
        tok = (("dma", chan), c[1])
        self._register(tok, reads, writes)
        return ins

    def barrier(self):
        for eng, e in self.E.items():
            kn = self.known[eng]
            for chan, (sem, val) in self.chan.items():
                if isinstance(chan, tuple) and chan[0] == "wsch":
                    continue
                key = ("dma", chan)
                if val > 0 and kn.get(key, 0) < val:
                    e.wait_ge(sem, val)
                    kn[key] = val
            for k in self.E:
                if k != eng and self.cnt[k] > 0 and kn.get(k, 0) < self.cnt[k]:
                    e.wait_ge(self.sem[k], self.cnt[k])
                    kn[k] = self.cnt[k]
        self.last_w = {k: v for k, v in self.last_w.items() if isinstance(k, tuple) and k[0] == "ws"}
        self.readers = {}

    def finish(self, eng="sp"):
        e = self.E[eng]
        for chan, (sem, val) in self.chan.items():
            if val > 0 and self.known[eng].get(("dma", chan), 0) < val:
                e.wait_ge(sem, val)
        for k in self.E:
            if k != eng and self.cnt[k] > 0:
                e.wait_ge(self.sem[k], self.cnt[k])


def blocks_of(n):
    out = []
    o = 0
    while o < n:
        out.append((o, min(128, n - o)))
        o += 128
    return out


def build_program(cfg=None):
    cfg = cfg or {}
    phases = cfg.get("phases", "WABCD")
    n_own_tiles = cfg.get("n_own_tiles", NTI)
    n_oth_tiles = cfg.get("n_oth_tiles", 9)
    heads = cfg.get("heads", list(range(H)))
    nkt_own = cfg.get("nkt_own", 32)
    nkt_oth = cfg.get("nkt_oth", 32)
    dump = cfg.get("dump", ())

    nc = bass.Bass("TRN2", target_bir_lowering=False)
    S = Sched(nc)

    def din(name, shape, dt=F32):
        return nc.dram_tensor(name, list(shape), dt, kind="ExternalInput").ap()

    def dscr(name, shape, dt=BF16):
        kind = "ExternalOutput" if name in dump else "Internal"
        return nc.dram_tensor(name, list(shape), dt, kind=kind).ap()

    xw = din("xw", [WIN, D])
    xo = din("xo", [NOTH, D])
    cs1w = din("cs1w", [64, WIN]); cs2w = din("cs2w", [64, WIN])
    cs1o = din("cs1o", [64, NOTH]); cs2o = din("cs2o", [64, NOTH])
    bidx_d = din("bidx", [128, WB])
    cvals_d = din("cvals", [128, 40])
    ident_d = din("ident", [128, 128])
    rel_bias = din("rel_bias", [32, 16])
    final_norm_g = din("final_norm_g", [D])
    rms_attn_g = din("rms_attn_g", [1, D])
    w_in = din("w_in", [1, D, IN_COLS])
    lam_d = [din(n, [1, 64]) for n in ("lambda_q1", "lambda_k1", "lambda_q2", "lambda_k2")]
    diff_subln_g = din("diff_subln_g", [1, 128])
    mla_q_norm_g = din("mla_q_norm_g", [1, 1024])
    w_mla_q_up = din("w_mla_q_up", [1, 1024, 3072])
    mla_kv_norm_g = din("mla_kv_norm_g", [1, 512])
    w_mla_kv_up = din("w_mla_kv_up", [1, 512, 4096])
    w_branch_a = din("w_branch_a", [1, 2048, D])
    w_branch_b = din("w_branch_b", [1, 2048, D])
    w_out = din("w_out", [1, D, D])
    rms_ffn_g = din("rms_ffn_g", [1, D])
    w_ffn_up = din("w_ffn_up", [1, D, 2 * DFF])
    conv_w = din("conv_w", [1, 3, 2 * DFF])
    conv_b = din("conv_b", [1, 2 * DFF])
    w_ffn_down = din("w_ffn_down", [1, DFF, D])
    y_out = nc.dram_tensor("y", [4096, D], F32, kind="ExternalOutput").ap()

    QT = dscr("QT", [H, 128, WIN]); KT = dscr("KT", [H, 128, NK]); VV = dscr("VV", [NK, 2048])
    QN = dscr("QN", [H, 128, WIN]); QPE = dscr("QPE", [64, H, WIN])
    KN = dscr("KN", [H, 128, NK]); KPE = dscr("KPE", [64, NK]); VM = dscr("VM", [NK, 2048])
    SG = dscr("SG", [64, 128, WIN])
    AO = dscr("AO", [H, 128, WIN]); BO = dscr("BO", [H, 128, WIN])
    BT = dscr("BT", [H, 128, WB], F32)
    X1 = dscr("X1", [WIN, D], F32)
    X2 = dscr("X2", [NTI * 464, D], F32)

    wsc = {}
    wconv = []

    def add_wblock(key, src2d, r0, kc, c0, width, dst=None, dcol=0, dwidth=None):
        if dst is None:
            dst = dscr("ws_" + "_".join(str(k) for k in key), [128, kc, dwidth or width])
            wsc[key] = dst
        for cg in range(0, kc, 8):
            n = min(8, kc - cg)
            src = src2d[r0 + cg * 128:r0 + (cg + n) * 128, c0:c0 + width].rearrange("(c p) m -> p c m", p=128)
            wconv.append((dst[:, cg:cg + n, dcol:dcol + width], src, key))
        return dst

    win2 = w_in[0]
    for j in range(4):
        add_wblock(("dq", j), win2, 0, 32, 512 * j, 512)
    for j in range(4):
        add_wblock(("dk", j), win2, 0, 32, 2048 + 512 * j, 512)
    for j in range(4):
        add_wblock(("dv", j), win2, 0, 32, 4096 + 512 * j, 512)
    for j in range(2):
        add_wblock(("ql", j), win2, 0, 32, 6144 + 512 * j, 512)
    add_wblock(("kvl", 0), win2, 0, 32, 7168, 512)
    add_wblock(("kpe", 0), win2, 0, 32, 7680, 64)
    for j in range(16):
        add_wblock(("gate", j), win2, 0, 32, 7744 + 512 * j, 512)
    for j in range(8):
        add_wblock(("qup", j), w_mla_q_up[0], 0, 8, 384 * j, 384)
    for j in range(8):
        add_wblock(("kvup", j), w_mla_kv_up[0], 0, 4, 512 * j, 512)
    for j in range(8):
        add_wblock(("wa", j), w_branch_a[0], 0, 16, 512 * j, 512)
        add_wblock(("wb", j), w_branch_b[0], 0, 16, 512 * j, 512)
    for j in range(8):
        add_wblock(("wo", j), w_out[0], 0, 32, 512 * j, 512)
    for j in range(86):
        dst = add_wblock(("up", j), w_ffn_up[0], 0, 32, 128 * j, 128, dwidth=256)
        add_wblock(("up", j), w_ffn_up[0], 0, 32, DFF + 128 * j, 128, dst=dst, dcol=128)
    NCG_DOWN = 11
    for n in range(8):
        for cg in range(NCG_DOWN):
            kc = min(8, 86 - cg * 8)
            add_wblock(("down", n, cg), w_ffn_down[0], cg * 1024, kc, 512 * n, 512)

    pb = [S.ps("pb%d" % k, [128, 512]) for k in range(8)]
    pbk = [("pb", k) for k in range(8)]
    ident = S.sb("ident", [128, 128], F32)
    ones_f = S.sb("ones_f", [128, 128], F32)
    ones_b = S.sb("ones_b", [128, 128], BF16)
    cvals = S.sb("cvals", [128, 40], F32)
    rbrep = S.sb("rbrep", [128, 33 * 16], F32)
    coth = S.sb("coth", [128, 16], F32)
    neglam = S.sb("neglam", [128, 1], F32)
    gsub = S.sb("gsub", [128, 1], F32)
    epsv = S.sb("epsv", [128, 1], F32)
    g_attn = S.sb("g_attn", [128, 32], F32)
    g_ffn = S.sb("g_ffn", [128, 32], F32)
    g_q = S.sb("g_q", [128, 8], F32)
    g_kv = S.sb("g_kv", [128, 4], F32)
    cw = S.sb("cw", [128, 3, 172], F32)
    cb = S.sb("cb", [128, 172], F32)
    ssq = S.sb("ssq", [128, 8], F32)
    rs = S.sb("rs", [128, 8], F32)

    S.dma("sp", ident[:], ident_d, writes=["ident"], chan="pro")
    S.dma("sp", cvals[:], cvals_d, writes=["cvals"], chan="pro")
    S.dma("sp", rbrep[:, 0:512], rel_bias.rearrange("b h -> (b h)").partition_broadcast(128), writes=["rbrep"], chan="pro")
    S.op("dve", lambda: nc.vector.memset(rbrep[:, 512:528], NEG), writes=["rbrep2"])
    S.op("dve", lambda: nc.vector.memset(ones_f[:], 1.0), writes=["ones_f"])
    S.op("dve", lambda: nc.vector.memset(ones_b[:], 1.0), writes=["ones_b"])
    S.op("dve", lambda: nc.vector.memset(epsv[:], EPS), writes=["epsv"])
    S.dma("sp", g_attn[:], rms_attn_g[0].rearrange("(c p) -> p c", p=128), writes=["g_attn"], chan="pro", allow_slow_non_contiguous=True)
    S.dma("sp", g_ffn[:], rms_ffn_g[0].rearrange("(c p) -> p c", p=128), writes=["g_ffn"], chan="pro", allow_slow_non_contiguous=True)
    S.dma("sp", g_q[:], mla_q_norm_g[0].rearrange("(c p) -> p c", p=128), writes=["g_q"], chan="pro", allow_slow_non_contiguous=True)
    S.dma("sp", g_kv[:], mla_kv_norm_g[0].rearrange("(c p) -> p c", p=128), writes=["g_kv"], chan="pro", allow_slow_non_contiguous=True)
    S.dma("sp", gsub[:], diff_subln_g[0].rearrange("(p o) -> p o", o=1), writes=["gsub"], chan="pro", allow_slow_non_contiguous=True)
    for k in range(3):
        S.dma("sp", cw[:, k, :], conv_w[0, k].rearrange("(c p) -> p c", p=128), writes=[("cw", k)], chan="pro", allow_slow_non_contiguous=True)
    S.dma("sp", cb[:], conv_b[0].rearrange("(c p) -> p c", p=128), writes=["cb"], chan="pro", allow_slow_non_contiguous=True)

    if "W" in phases:
        wsel = cfg.get("wkeys")
        for dst, src, key in wconv:
            if wsel is not None and key[0] not in wsel:
                continue
            S.dma("pool", dst, src, writes=[("ws", key)], chan=("wsch", key[0]))
        for dst, src, key in wconv:
            ch = ("wsch", key[0])
            if ch in S.chan:
                S.last_w[("ws", key)] = (("dma", ch), S.chan[ch][1])

    with ExitStack() as st0:
        lt = [S.sb("lamt%d" % k, [128, 64], F32, st0) for k in range(4)]
        lj = S.sb("lamj", [128, 64], F32, st0)
        ld = S.sb("lamd", [128, 2], F32, st0)
        for k in range(4):
            S.dma("sp", lt[k][:], lam_d[k][0].partition_broadcast(128), writes=[("lamt", k)], chan="pro")
        S.barrier()
        S.op("dve", lambda: nc.vector.tensor_scalar(out=gsub[:], in0=gsub[:], scalar1=0.8, scalar2=None, op0=ALU.mult),
             reads=["gsub"], writes=["gsub"])
        for k in range(2):
            S.op("dve", lambda: nc.vector.scalar_tensor_tensor(out=lj[:], in0=lt[2 * k][:], scalar=1.0, in1=lt[2 * k + 1][:],
                                                                op0=ALU.mult, op1=ALU.mult, accum_out=ld[:, k:k + 1]),
                 reads=[("lamt", 2 * k), ("lamt", 2 * k + 1)], writes=["lamj", ("lamd", k)])
        S.op("act", lambda: nc.scalar.activation(out=ld[:], in_=ld[:], func=AF.Exp), reads=[("lamd", 0), ("lamd", 1)], writes=[("lamd", 0), ("lamd", 1)])
        S.op("dve", lambda: nc.vector.tensor_tensor(out=neglam[:], in0=ld[:, 1:2], in1=ld[:, 0:1], op=ALU.subtract),
             reads=[("lamd", 0), ("lamd", 1)], writes=["neglam"])
        S.op("dve", lambda: nc.vector.tensor_scalar(out=neglam[:], in0=neglam[:], scalar1=-0.2, scalar2=None, op0=ALU.add),
             reads=["neglam"], writes=["neglam"])
        S.op("dve", lambda: nc.vector.memset(coth[:], 0.0), writes=["coth"])
        for b in range(33):
            S.op("dve", lambda: nc.vector.scalar_tensor_tensor(out=coth[:], in0=rbrep[:, 16 * b:16 * b + 16], scalar=cvals[:, b:b + 1],
                                                                in1=coth[:], op0=ALU.mult, op1=ALU.add),
                 reads=["rbrep", "rbrep2", "cvals", "coth"], writes=["coth"])
        S.barrier()

    if "B" in phases:
        with ExitStack() as st0:
            bidx = S.sb("bidx", [128, WB], F32, st0)
            btmp = S.sb("btmp", [128, WB], F32, st0)
            bacc = [S.sb("bacc%d" % k, [128, WB], F32, st0) for k in range(2)]
            S.dma("sp", bidx[:], bidx_d, writes=["bidx"], chan="pro")
            S.barrier()
            for hi, h in enumerate(heads):
                ba = bacc[hi % 2]
                bk = ("bacc", hi % 2)
                S.op("dve", lambda: nc.vector.memset(ba[:], 0.0), writes=[bk])
                for b in range(33):
                    S.op("dve", lambda: nc.vector.tensor_scalar(out=btmp[:], in0=bidx[:], scalar1=float(b), scalar2=rbrep[:, 16 * b + h:16 * b + h + 1],
                                                                 op0=ALU.is_equal, op1=ALU.mult),
                         reads=["bidx", "rbrep", "rbrep2"], writes=["btmp"])
                    S.op("dve", lambda: nc.vector.tensor_tensor(out=ba[:], in0=ba[:], in1=btmp[:], op=ALU.add),
                         reads=["btmp", bk], writes=[bk])
                S.dma("sp", BT[h], ba[:], reads=[bk], writes=[("BT", h)], chan=("btst", hi % 2))
            S.barrier()

    bank_rr = [0]

    def next_bank():
        k = bank_rr[0]
        bank_rr[0] = (k + 1) % 8
        return pb[k], pbk[k]

    def rstd_from_ssq(ap_io, n_part, inv_n, rkeys, wkeys):
        S.op("act", lambda: nc.scalar.activation(out=ap_io, in_=ap_io, func=AF.Ln, scale=inv_n, bias=epsv[:n_part, :]),
             reads=list(rkeys) + ["epsv"], writes=wkeys)
        S.op("act", lambda: nc.scalar.activation(out=ap_io, in_=ap_io, func=AF.Exp, scale=-0.5),
             reads=wkeys, writes=wkeys)

    def make_hT(rows_ap, Tn, gvec, gkey, hT, hkey, xt_list, extra_read=()):
        for bi, (off, n) in enumerate(blocks_of(Tn)):
            xs, xkey = xt_list[bi % len(xt_list)]
            S.dma("sp", xs[:n, :], rows_ap[off:off + n, :], reads=list(extra_read), writes=[xkey])
            sq = ssq[:n, bi % 8:bi % 8 + 1]
            rr = rs[:n, bi % 8:bi % 8 + 1]
            sk = ("ssq", bi % 8)
            rk = ("rs", bi % 8)
            S.op("dve", lambda: nc.vector.scalar_tensor_tensor(out=junk[:n, :], in0=xs[:n, :], scalar=1.0, in1=xs[:n, :],
                                                                op0=ALU.mult, op1=ALU.mult, accum_out=sq),
                 reads=[xkey], writes=["junk", sk])
            S.op("act", lambda: nc.scalar.activation(out=rr, in_=sq, func=AF.Ln, scale=1.0 / D, bias=epsv[:n, :]),
                 reads=[sk, "epsv"], writes=[rk])
            S.op("act", lambda: nc.scalar.activation(out=rr, in_=rr, func=AF.Exp, scale=-0.5), reads=[rk], writes=[rk])
            S.op("dve", lambda: nc.vector.tensor_scalar(out=xs[:n, :], in0=xs[:n, :], scalar1=rr, scalar2=None, op0=ALU.mult),
                 reads=[xkey, rk], writes=[xkey])
            for c4 in range(8):
                bank, bkey = next_bank()
                for k in range(4):
                    c = c4 * 4 + k
                    S.op("pe", lambda: nc.tensor.transpose(out=bank[:, k * 128:k * 128 + n], in_=xs[:n, c * 128:(c + 1) * 128],
                                                           identity=ident[:n, :n]),
                         reads=[xkey, "ident"], writes=[bkey], sig=(k == 3))
                S.op("dve", lambda: nc.vector.tensor_tensor(
                    out=hT[:, c4 * 4:c4 * 4 + 4, off:off + n],
                    in0=bank[:, 0:512].rearrange("p (k t) -> p k t", k=4)[:, :, :n],
                    in1=gvec[:, c4 * 4:c4 * 4 + 4].unsqueeze(2).broadcast_to([128, 4, n]), op=ALU.mult),
                    reads=[bkey, gkey], writes=[hkey])

    wslot_rr = [0]

    def load_w(wslots, key, kc, width):
        s = wslot_rr[0] % len(wslots)
        wslot_rr[0] += 1
        wt, wk = wslots[s]
        view = wt[:, 0:kc * width].rearrange("p (c m) -> p c m", c=kc)
        S.dma("sp", view, wsc[key], reads=[("ws", key)], writes=[wk])
        return view, wk

    def f_proj(wv, wk, kc, m0, mw, rhsT, rkey, Tn, p0=0):
        bank, bkey = next_bank()
        for c in range(kc):
            S.op("pe", lambda: nc.tensor.matmul(bank[:mw, :Tn], lhsT=wv[:, c, m0:m0 + mw], rhs=rhsT[:, c, :Tn],
                                                start=(c == 0), stop=(c == kc - 1)),
                 reads=[wk, rkey], writes=[bkey], sig=(c == kc - 1))
        return bank, bkey

    def t_proj(wv, wk, kc, n0, nw, lhsT, lkey, off, n):
        bank, bkey = next_bank()
        for c in range(kc):
            S.op("pe", lambda: nc.tensor.matmul(bank[:n, :nw], lhsT=lhsT[:, c, off:off + n], rhs=wv[:, c, n0:n0 + nw],
                                                start=(c == 0), stop=(c == kc - 1)),
                 reads=[wk, lkey], writes=[bkey], sig=(c == kc - 1))
        return bank, bkey

    stg_rr = [0]

    def stage_store(stgs, bank, bkey, np_, ncol, dst, dkey, act_func=None):
        s = stg_rr[0] % len(stgs)
        stg_rr[0] += 1
        st, sk = stgs[s]
        if act_func is None:
            S.op("dve", lambda: nc.vector.tensor_copy(out=st[:np_, :ncol], in_=bank[:np_, :ncol]), reads=[bkey], writes=[sk])
        else:
            S.op("act", lambda: nc.scalar.activation(out=st[:np_, :ncol], in_=bank[:np_, :ncol], func=act_func), reads=[bkey], writes=[sk])
        S.dma("pool", dst, st[:np_, :ncol], reads=[sk], writes=[dkey], chan=("stg", s))

    def key_range(i):
        a = max(T * i, 1)
        b = min(T * i + T, 4097)
        return a - T * i, b - T * i, a - 1

    if "A" in phases:
        with ExitStack() as stA:
            xts = [(S.sb("xtA%d" % k, [128, D], F32, stA), ("xt", k)) for k in range(1)]
            junk = S.sb("junkA", [128, D], BF16, stA)
            hTs = [(S.sb("hTA%d" % k, [128, 32, 464], BF16, stA), ("hT", k)) for k in range(1)]
            wslots = [(S.sb("wsA%d" % k, [128, 32 * 512], BF16, stA), ("wslot", k)) for k in range(2)]
            stgs = [(S.sb("stgA%d" % k, [128, 512], BF16, stA), ("stgb", k)) for k in range(4)]
            qlat = S.sb("qlat", [128, 8, 464], F32, stA)
            kvlat = S.sb("kvlat", [128, 4, 464], F32, stA)
            sqt = S.sb("sqt", [128, 464], F32, stA)
            rrep = S.sb("rrep", [128, 464], F32, stA)
            qn = S.sb("qn", [128, 8, 464], BF16, stA)
            kvn = S.sb("kvn", [128, 4, 464], BF16, stA)
            pe_as = [(S.sb("pe_a%d" % k, [64, 464], F32, stA), ("pe_a", k)) for k in range(2)]
            pe_ss = [(S.sb("pe_s%d" % k, [64, 464], F32, stA), ("pe_s", k)) for k in range(2)]
            cs1 = S.sb("cs1", [64, 464], F32, stA)
            cs2 = S.sb("cs2", [64, 464], F32, stA)
            rope_rr = [0]

            def rope_store(bank, bkey, Tn, dst_fn):
                r = rope_rr[0] % 2
                rope_rr[0] += 1
                pa, pak = pe_as[r]
                ps_, psk = pe_ss[r]
                S.op("dve", lambda: nc.vector.tensor_copy(out=pa[:, :Tn], in_=bank[:64, :Tn]), reads=[bkey], writes=[pak])
                S.dma("sp", ps_[0:32, :Tn], pa[32:64, :Tn], reads=[pak], writes=[(psk, 0)], chan=("ropesw", r, 0))
                S.dma("sp", ps_[32:64, :Tn], pa[0:32, :Tn], reads=[pak], writes=[(psk, 1)], chan=("ropesw", r, 1))
                S.op("dve", lambda: nc.vector.tensor_tensor(out=pa[:, :Tn], in0=pa[:, :Tn], in1=cs1[:, :Tn], op=ALU.mult),
                     reads=[pak, "cs1"], writes=[pak])
                S.op("dve", lambda: nc.vector.tensor_tensor(out=ps_[:, :Tn], in0=ps_[:, :Tn], in1=cs2[:, :Tn], op=ALU.mult),
                     reads=[(psk, 0), (psk, 1), "cs2"], writes=[(psk, 0), (psk, 1)])
                s_ = stg_rr[0] % len(stgs); stg_rr[0] += 1
                st, sk = stgs[s_]
                S.op("dve", lambda: nc.vector.tensor_tensor(out=st[:64, :Tn], in0=pa[:, :Tn], in1=ps_[:, :Tn], op=ALU.add),
                     reads=[pak, (psk, 0), (psk, 1)], writes=[sk])
                dst_fn(st, sk, s_)

            tiles = [("own", i) for i in range(n_own_tiles)] + [("oth", i) for i in range(n_oth_tiles)]
            for ti, (kind, i) in enumerate(tiles):
                own = kind == "own"
                t0 = i * T
                Tn = T if own else min(T, NOTH - t0)
                rows = (xw if own else xo)[t0:t0 + Tn, :]
                hT, hkey = hTs[0]
                make_hT(rows, Tn, g_attn, "g_attn", hT, hkey, xts)
                if own:
                    c0, c1, k0 = key_range(i)
                else:
                    c0, c1, k0 = 0, Tn, 4096 + t0
                nkeys = c1 - c0
                S.dma("sp", cs1[:, :Tn], (cs1w if own else cs1o)[:, t0:t0 + Tn], writes=["cs1"])
                S.dma("sp", cs2[:, :Tn], (cs2w if own else cs2o)[:, t0:t0 + Tn], writes=["cs2"])
                for name, dst in ((("dq", QT),) if own else ()) + (("dk", KT),):
                    for j in range(4):
                        wv, wk = load_w(wslots, (name, j), 32, 512)
                        for m in range(4):
                            h = 4 * j + m
                            bank, bkey = f_proj(wv, wk, 32, 128 * m, 128, hT, hkey, Tn)
                            if name == "dq":
                                stage_store(stgs, bank, bkey, 128, Tn, dst[h, :, t0:t0 + Tn], ("QT", h))
                            elif nkeys > 0:
                                bb = bank[:, c0:c1]
                                s = stg_rr[0] % len(stgs); stg_rr[0] += 1
                                st, sk = stgs[s]
                                S.op("dve", lambda: nc.vector.tensor_copy(out=st[:, :nkeys], in_=bb), reads=[bkey], writes=[sk])
                                S.dma("pool", dst[h, :, k0:k0 + nkeys], st[:, :nkeys], reads=[sk], writes=[("KT", h)], chan=("stg", s))
                for j in range(4):
                    wv, wk = load_w(wslots, ("dv", j), 32, 512)
                    for (off, n) in blocks_of(Tn):
                        bank, bkey = t_proj(wv, wk, 32, 0, 512, hT, hkey, off, n)
                        a = max(off, c0); b = min(off + n, c1)
                        if b > a:
                            s = stg_rr[0] % len(stgs); stg_rr[0] += 1
                            st, sk = stgs[s]
                            S.op("dve", lambda: nc.vector.tensor_copy(out=st[:n, :], in_=bank[:n, :]), reads=[bkey], writes=[sk])
                            S.dma("pool", VV[k0 + a - c0:k0 + b - c0, 512 * j:512 * j + 512], st[a - off:b - off, :],
                                  reads=[sk], writes=[("VV", j)], chan=("stg", s))
                lat_list = ([("ql", 2, qlat, "qlat")] if own else []) + [("kvl", 1, kvlat, "kvlat")]
                for name, nb, lat, lkey in lat_list:
                    for j in range(nb):
                        wv, wk = load_w(wslots, (name, j), 32, 512)
                        for m in range(4):
                            bank, bkey = f_proj(wv, wk, 32, 128 * m, 128, hT, hkey, Tn)
                            S.op("dve", lambda: nc.vector.tensor_copy(out=lat[:, 4 * j + m, :Tn], in_=bank[:, :Tn]), reads=[bkey], writes=[lkey])
                wv, wk = load_w(wslots, ("kpe", 0), 32, 64)
                bank, bkey = f_proj(wv, wk, 32, 0, 64, hT, hkey, Tn)

                def kpe_dst(st, sk, s_):
                    if nkeys > 0:
                        S.dma("pool", KPE[:, k0:k0 + nkeys], st[:64, c0:c1], reads=[sk], writes=["KPE"], chan=("stg", s_))
                rope_store(bank, bkey, Tn, kpe_dst)
                if own:
                    for j in range(16):
                        wv, wk = load_w(wslots, ("gate", j), 32, 512)
                        for m in range(4):
                            bank, bkey = f_proj(wv, wk, 32, 128 * m, 128, hT, hkey, Tn)
                            stage_store(stgs, bank, bkey, 128, Tn, SG[4 * j + m, :, t0:t0 + Tn], ("SG", 4 * j + m), act_func=AF.Sigmoid)
                for name, nch, lat, lkey, gv, gk, outn, okey in (([("q", 8, qlat, "qlat", g_q, "g_q", qn, "qn")] if own else [])
                                                                 + [("kv", 4, kvlat, "kvlat", g_kv, "g_kv", kvn, "kvn")]):
                    bank, bkey = next_bank()
                    for c in range(nch):
                        S.op("dve", lambda: nc.vector.tensor_tensor(out=sqt[:, :Tn], in0=lat[:, c, :Tn], in1=lat[:, c, :Tn], op=ALU.mult),
                             reads=[lkey], writes=["sqt"])
                        S.op("pe", lambda: nc.tensor.matmul(bank[:, :Tn], lhsT=ones_f[:], rhs=sqt[:, :Tn], start=(c == 0), stop=(c == nch - 1)),
                             reads=["ones_f", "sqt"], writes=[bkey], sig=True)
                    S.op("act", lambda: nc.scalar.activation(out=rrep[:, :Tn], in_=bank[:, :Tn], func=AF.Ln, scale=1.0 / (128 * nch), bias=epsv[:]),
                         reads=[bkey, "epsv"], writes=["rrep"])
                    S.op("act", lambda: nc.scalar.activation(out=rrep[:, :Tn], in_=rrep[:, :Tn], func=AF.Exp, scale=-0.5), reads=["rrep"], writes=["rrep"])
                    for c in range(nch):
                        S.op("dve", lambda: nc.vector.scalar_tensor_tensor(out=outn[:, c, :Tn], in0=lat[:, c, :Tn], scalar=gv[:, c:c + 1], in1=rrep[:, :Tn],
                                                                            op0=ALU.mult, op1=ALU.mult),
                             reads=[lkey, gk, "rrep"], writes=[okey])
                if own:
                    for j in range(8):
                        wv, wk = load_w(wslots, ("qup", j), 8, 384)
                        for m in range(2):
                            h = 2 * j + m
                            bank, bkey = f_proj(wv, wk, 8, 192 * m, 128, qn, "qn", Tn)
                            stage_store(stgs, bank, bkey, 128, Tn, QN[h, :, t0:t0 + Tn], ("QN", h))
                            bank, bkey = f_proj(wv, wk, 8, 192 * m + 128, 64, qn, "qn", Tn)

                            def qpe_dst(st, sk, s_, h=h):
                                S.dma("pool", QPE[:, h, t0:t0 + Tn], st[:64, :Tn], reads=[sk], writes=[("QPE", h)], chan=("stg", s_))
                            rope_store(bank, bkey, Tn, qpe_dst)
                for j in range(8):
                    wv, wk = load_w(wslots, ("kvup", j), 4, 512)
                    for m in range(2):
                        h = 2 * j + m
                        bank, bkey = f_proj(wv, wk, 4, 256 * m, 128, kvn, "kvn", Tn)
                        if nkeys > 0:
                            bb = bank[:, c0:c1]
                            s = stg_rr[0] % len(stgs); stg_rr[0] += 1
                            st, sk = stgs[s]
                            S.op("dve", lambda: nc.vector.tensor_copy(out=st[:, :nkeys], in_=bb), reads=[bkey], writes=[sk])
                            S.dma("pool", KN[h, :, k0:k0 + nkeys], st[:, :nkeys], reads=[sk], writes=[("KN", h)], chan=("stg", s))
                        for (off, n) in blocks_of(Tn):
                            bank, bkey = t_proj(wv, wk, 4, 256 * m + 128, 128, kvn, "kvn", off, n)
                            a = max(off, c0); b = min(off + n, c1)
                            if b > a:
                                s = stg_rr[0] % len(stgs); stg_rr[0] += 1
                                st, sk = stgs[s]
                                S.op("dve", lambda: nc.vector.tensor_copy(out=st[:n, :128], in_=bank[:n, :128]), reads=[bkey], writes=[sk])
                                S.dma("pool", VM[k0 + a - c0:k0 + b - c0, 128 * h:128 * h + 128], st[a - off:b - off, :128],
                                      reads=[sk], writes=[("VM", h)], chan=("stg", s))
            S.barrier()

    if "B" in phases:
        with ExitStack() as stB:
            kt_sb = S.sb("kt_sb", [128, NK], BF16, stB)
            v_sb = S.sb("v_sb", [128, 64, 128], BF16, stB)
            kn_sb = S.sb("kn_sb", [128, NK], BF16, stB)
            vm_sb = S.sb("vm_sb", [128, 64, 128], BF16, stB)
            kpe_sb = S.sb("kpe_sb", [64, NK], BF16, stB)
            bts = [(S.sb("bt%d" % k, [128, WB], F32, stB), ("bt", k)) for k in range(2)]
            qts = [(S.sb("qt%d" % k, [128, T], BF16, stB), ("qt", k)) for k in range(2)]
            qns = [(S.sb("qnb%d" % k, [128, T], BF16, stB), ("qnb", k)) for k in range(2)]
            qps = [(S.sb("qpb%d" % k, [64, T], BF16, stB), ("qpb", k)) for k in range(2)]
            NP = 12
            pts = [(S.sb("pt%d" % k, [128, T], BF16, stB), ("pt", k)) for k in range(NP)]
            tmps = [(S.sb("tmpb%d" % k, [128, T], F32, stB), ("tmpb", k)) for k in range(2)]
            accs = [[(S.sb("acc%d_%d" % (k, c), [128, T], F32, stB), ("acc", k, c)) for c in range(2)] for k in range(2)]
            e0 = S.sb("e0", [128, T], F32, stB)
            e1 = S.sb("e1", [128, T], F32, stB)
            fa = S.sb("fa", [128, T], F32, stB)
            fb = S.sb("fb", [128, T], F32, stB)
            fc = S.sb("fc", [128, T], F32, stB)
            ost = [(S.sb("ost%d" % k, [128, T], BF16, stB), ("ost", k)) for k in range(2)]
            S.dma("sp", kpe_sb[:], KPE, writes=["kpe_sb"])
            prr = [0]
            trr = [0]
            orr = [0]
            ktiles = list(range(nkt_own)) + [32 + jj for jj in range(nkt_oth)]
            NKT = len(ktiles)
            vkeys = [("v_sb", q4) for q4 in range(4)]
            vmkeys = [("vm_sb", q4) for q4 in range(4)]

            items = []
            for hi, h in enumerate(heads):
                for i in range(n_own_tiles):
                    items.append({"hi": hi, "h": h, "kind": "diff", "i": i})
                for i in range(n_own_tiles):
                    items.append({"hi": hi, "h": h, "kind": "mla", "i": i})
            for n_, it in enumerate(items):
                it["n"] = n_

            def prep(it):
                h, i, n_ = it["h"], it["i"], it["n"]
                if it["kind"] == "diff":
                    it["bt"], it["btk"] = bts[it["hi"] % 2]
                    if i == 0:
                        S.dma("sp", it["bt"][:], BT[h], writes=[it["btk"]])
                        S.dma("sp", kt_sb[:], KT[h], writes=["kt_sb"])
                        for q4 in range(4):
                            S.dma("sp", v_sb[:, 16 * q4:16 * q4 + 16, :],
                                  VV[2048 * q4:2048 * q4 + 2048, 128 * h:128 * h + 128].rearrange("(j p) e -> p j e", p=128),
                                  writes=[vkeys[q4]], chan=vkeys[q4])
                    it["qt"], it["qk"] = qts[n_ % 2]
                    S.dma("sp", it["qt"][:], QT[h, :, T * i:T * i + T], writes=[it["qk"]])
                else:
                    if i == 0:
                        S.dma("sp", kn_sb[:], KN[h], writes=["kn_sb"])
                        for q4 in range(4):
                            S.dma("sp", vm_sb[:, 16 * q4:16 * q4 + 16, :],
                                  VM[2048 * q4:2048 * q4 + 2048, 128 * h:128 * h + 128].rearrange("(j p) e -> p j e", p=128),
                                  writes=[vmkeys[q4]], chan=vmkeys[q4])
                    it["qn"], it["qnk"] = qns[n_ % 2]
                    it["qp"], it["qpk"] = qps[n_ % 2]
                    S.dma("sp", it["qn"][:], QN[h, :, T * i:T * i + T], writes=[it["qnk"]])
                    S.dma("sp", it["qp"][:], QPE[:, h, T * i:T * i + T], writes=[it["qpk"]])
                it["acc"] = accs[n_ % 2]

            gs = [0]

            def emit_S(it, ji):
                j = ktiles[ji]
                b0 = 2 * (gs[0] % 2)
                gs[0] += 1
                it.setdefault("sbank", {})[ji] = b0
                if it["kind"] == "diff":
                    qt, qk = it["qt"], it["qk"]
                    S.op("pe", lambda: nc.tensor.matmul(pb[b0][:, :T], lhsT=kt_sb[0:64, 128 * j:128 * j + 128], rhs=qt[0:64, :], start=True, stop=True),
                         reads=["kt_sb", qk], writes=[pbk[b0]])
                    S.op("pe", lambda: nc.tensor.matmul(pb[b0 + 1][:, :T], lhsT=kt_sb[64:128, 128 * j:128 * j + 128], rhs=qt[64:128, :], start=True, stop=True),
                         reads=["kt_sb", qk], writes=[pbk[b0 + 1]])
                else:
                    S.op("pe", lambda: nc.tensor.matmul(pb[b0][:, :T], lhsT=kn_sb[:, 128 * j:128 * j + 128], rhs=it["qn"][:], start=True, stop=False),
                         reads=["kn_sb", it["qnk"]], writes=[pbk[b0]], sig=False)
                    S.op("pe", lambda: nc.tensor.matmul(pb[b0][:, :T], lhsT=kpe_sb[:, 128 * j:128 * j + 128], rhs=it["qp"][:], start=False, stop=True),
                         reads=["kpe_sb", it["qpk"]], writes=[pbk[b0]])

            def exp_tile(sbank, sbkey, scale, mode, bias_ap, bias_keys):
                pt, pk = pts[prr[0] % NP]
                prr[0] += 1
                if mode == "const":
                    S.op("act", lambda: nc.scalar.activation(out=pt[:], in_=sbank[:, :T], func=AF.Exp, scale=scale, bias=bias_ap),
                         reads=[sbkey] + bias_keys, writes=[pk])
                else:
                    tm, tk = tmps[trr[0] % 2]
                    trr[0] += 1
                    S.op("dve", lambda: nc.vector.scalar_tensor_tensor(out=tm[:], in0=sbank[:, :T], scalar=scale, in1=bias_ap,
                                                                        op0=ALU.mult, op1=ALU.add),
                         reads=[sbkey] + bias_keys, writes=[tk])
                    S.op("act", lambda: nc.scalar.activation(out=pt[:], in_=tm[:], func=AF.Exp), reads=[tk], writes=[pk])
                return pt, pk

            def accum(acc, acck, p, pk, first):
                if first:
                    S.op("dve", lambda: nc.vector.tensor_copy(out=acc[:], in_=p[:]), reads=[pk], writes=[acck])
                else:
                    S.op("dve", lambda: nc.vector.tensor_tensor(out=acc[:], in0=acc[:], in1=p[:], op=ALU.add), reads=[pk, acck], writes=[acck])

            def emit_rest(it, ji):
                j = ktiles[ji]
                h, i = it["h"], it["i"]
                first = ji == 0
                last = ji == NKT - 1
                if it["kind"] == "diff":
                    bt, btk = it["bt"], it["btk"]
                    if j < 32:
                        dl = 128 * j - T * i + 1
                        if dl > U0:
                            mode, bap, bkeys = "const", rbrep[:, 16 * 31 + h:16 * 31 + h + 1], ["rbrep"]
                        elif dl < -217:
                            mode, bap, bkeys = "const", rbrep[:, 16 * 15 + h:16 * 15 + h + 1], ["rbrep"]
                        else:
                            u0 = U0 - dl
                            mode, bap, bkeys = "tile", bt[:, u0:u0 + T], [btk]
                    elif i == 8 and j == 32:
                        mode, bap, bkeys = "tile", bt[:, WT:WT + T], [btk]
                    elif i == 0 and j == 63:
                        mode, bap, bkeys = "tile", bt[:, WT + T:WT + 2 * T], [btk]
                    else:
                        mode, bap, bkeys = "const", coth[:, h:h + 1], ["coth"]
                    b0 = it["sbank"][ji]
                    p0, pk0 = exp_tile(pb[b0], pbk[b0], SC_DIFF, mode, bap, bkeys)
                    p1, pk1 = exp_tile(pb[b0 + 1], pbk[b0 + 1], SC_DIFF, mode, bap, bkeys)
                    vk = [vkeys[j // 16]]
                    S.op("pe", lambda: nc.tensor.matmul(pb[4][:, :T], lhsT=v_sb[:, j, :], rhs=p0[:], start=first, stop=last), reads=vk + [pk0], writes=[pbk[4]], sig=True)
                    S.op("pe", lambda: nc.tensor.matmul(pb[5][:, :T], lhsT=v_sb[:, j, :], rhs=p1[:], start=first, stop=last), reads=vk + [pk1], writes=[pbk[5]], sig=True)
                    accum(it["acc"][0][0], it["acc"][0][1], p0, pk0, first)
                    accum(it["acc"][1][0], it["acc"][1][1], p1, pk1, first)
                else:
                    b0 = it["sbank"][ji]
                    if j < 32:
                        p, pk = exp_tile(pb[b0], pbk[b0], SC_MLA, "const", 0.0, [])
                    else:
                        p, pk = exp_tile(pb[b0], pbk[b0], SC_MLA, "const", cvals[:, 33:34], ["cvals"])
                    S.op("pe", lambda: nc.tensor.matmul(pb[4][:, :T], lhsT=vm_sb[:, j, :], rhs=p[:], start=first, stop=last), reads=[vmkeys[j // 16], pk], writes=[pbk[4]], sig=True)
                    accum(it["acc"][0][0], it["acc"][0][1], p, pk, first)

            deferred = []

            def epilogue(it):
                h, i = it["h"], it["i"]
                (a0, a0k), (a1, a1k) = it["acc"]
                if it["kind"] == "diff":
                    S.op("dve", lambda: nc.vector.tensor_copy(out=e0[:], in_=pb[4][:, :T]), reads=[pbk[4]], writes=["e0"])
                    S.op("dve", lambda: nc.vector.tensor_copy(out=e1[:], in_=pb[5][:, :T]), reads=[pbk[5]], writes=["e1"])
                    S.op("pe", lambda: nc.tensor.matmul(pb[6][:, :T], lhsT=ones_f[:], rhs=a0[:], start=True, stop=True), reads=["ones_f", a0k], writes=[pbk[6]])
                    S.op("pe", lambda: nc.tensor.matmul(pb[7][:, :T], lhsT=ones_f[:], rhs=a1[:], start=True, stop=True), reads=["ones_f", a1k], writes=[pbk[7]])
                    S.op("dve", lambda: nc.vector.reciprocal(out=fa[:], in_=pb[6][:, :T]), reads=[pbk[6]], writes=["fa"])
                    S.op("dve", lambda: nc.vector.tensor_tensor(out=fa[:], in0=e0[:], in1=fa[:], op=ALU.mult), reads=["e0", "fa"], writes=["fa"])
                    S.op("dve", lambda: nc.vector.reciprocal(out=fb[:], in_=pb[7][:, :T]), reads=[pbk[7]], writes=["fb"])
                    S.op("dve", lambda: nc.vector.tensor_tensor(out=fb[:], in0=e1[:], in1=fb[:], op=ALU.mult), reads=["e1", "fb"], writes=["fb"])
                    S.op("dve", lambda: nc.vector.scalar_tensor_tensor(out=fa[:], in0=fb[:], scalar=neglam[:], in1=fa[:], op0=ALU.mult, op1=ALU.add),
                         reads=["fa", "fb", "neglam"], writes=["fa"])
                    S.op("dve", lambda: nc.vector.tensor_tensor(out=fb[:], in0=fa[:], in1=fa[:], op=ALU.mult), reads=["fa"], writes=["fb"])

                    def part2():
                        S.op("pe", lambda: nc.tensor.matmul(pb[6][:, :T], lhsT=ones_f[:], rhs=fb[:], start=True, stop=True), reads=["ones_f", "fb"], writes=[pbk[6]])
                        S.op("act", lambda: nc.scalar.activation(out=fc[:], in_=pb[6][:, :T], func=AF.Ln, scale=1.0 / 128, bias=epsv[:]),
                             reads=[pbk[6], "epsv"], writes=["fc"])
                        S.op("act", lambda: nc.scalar.activation(out=fc[:], in_=fc[:], func=AF.Exp, scale=-0.5), reads=["fc"], writes=["fc"])
                        os_, osk = ost[orr[0] % 2]
                        orr[0] += 1
                        S.op("dve", lambda: nc.vector.scalar_tensor_tensor(out=os_[:], in0=fa[:], scalar=gsub[:], in1=fc[:], op0=ALU.mult, op1=ALU.mult),
                             reads=["fa", "fc", "gsub"], writes=[osk])
                        S.dma("pool", AO[h, :, T * i:T * i + T], os_[:], reads=[osk], writes=[("AO", h)], chan=("ost", (orr[0] - 1) % 2))
                    deferred.append(part2)
                else:
                    S.op("dve", lambda: nc.vector.tensor_copy(out=e0[:], in_=pb[4][:, :T]), reads=[pbk[4]], writes=["e0"])
                    S.op("pe", lambda: nc.tensor.matmul(pb[6][:, :T], lhsT=ones_f[:], rhs=a0[:], start=True, stop=True), reads=["ones_f", a0k], writes=[pbk[6]])
                    S.op("dve", lambda: nc.vector.reciprocal(out=fc[:], in_=pb[6][:, :T]), reads=[pbk[6]], writes=["fc"])
                    os_, osk = ost[orr[0] % 2]
                    orr[0] += 1
                    S.op("dve", lambda: nc.vector.tensor_tensor(out=os_[:], in0=e0[:], in1=fc[:], op=ALU.mult), reads=["e0", "fc"], writes=[osk])
                    S.dma("pool", BO[h, :, T * i:T * i + T], os_[:], reads=[osk], writes=[("BO", h)], chan=("ost", (orr[0] - 1) % 2))

            if items:
                prep(items[0])
                emit_S(items[0], 0)
            for n_, it in enumerate(items):
                nxt = items[n_ + 1] if n_ + 1 < len(items) else None
                for ji in range(NKT):
                    if ji + 1 < NKT:
                        emit_S(it, ji + 1)
                    elif nxt is not None:
                        prep(nxt)
                        emit_S(nxt, 0)
                    emit_rest(it, ji)
                    if ji == min(8, NKT - 1) and deferred:
                        for fn in deferred:
                            fn()
                        deferred.clear()
                epilogue(it)
            for fn in deferred:
                fn()
            deferred.clear()
            S.barrier()

    if "C" in phases:
        with ExitStack() as stC:
            a_sb = S.sb("a_sb", [128, 16, T], BF16, stC)
            b_sb = S.sb("b_sb", [128, 16, T], BF16, stC)
            sgs = [(S.sb("sg%d" % k, [128, 2, T], BF16, stC), ("sg", k)) for k in range(2)]
            mT = S.sb("mT", [128, 32, T], BF16, stC)
            wslots = [(S.sb("wsC%d" % k, [128, 32 * 512], BF16, stC), ("wslot", k)) for k in range(2)]
            t1 = S.sb("t1c", [128, T], F32, stC)
            t2 = S.sb("t2c", [128, T], F32, stC)
            xbs = [(S.sb("xb%d" % k, [128, 512], F32, stC), ("xb", k)) for k in range(3)]
            xrr = [0]
            for i in range(n_own_tiles):
                t0 = T * i
                for hh in range(0, 16, 4):
                    S.dma("sp", a_sb[:, hh:hh + 4, :], AO[hh:hh + 4, :, t0:t0 + T].rearrange("h p t -> p h t"), writes=[("a_sb", hh)], chan=("a_sb", hh))
                    S.dma("sp", b_sb[:, hh:hh + 4, :], BO[hh:hh + 4, :, t0:t0 + T].rearrange("h p t -> p h t"), writes=[("b_sb", hh)], chan=("b_sb", hh))
                akeys = [("a_sb", hh) for hh in range(0, 16, 4)]
                bkeys_ = [("b_sb", hh) for hh in range(0, 16, 4)]
                for j in range(8):
                    wva, wka = load_w(wslots[0:1], ("wa", j), 16, 512)
                    wvb, wkb = load_w(wslots[1:2], ("wb", j), 16, 512)
                    for m in range(4):
                        mm_ = 4 * j + m
                        sg, sgk = sgs[mm_ % 2]
                        S.dma("sp", sg[:, 0, :], SG[mm_, :, t0:t0 + T], writes=[sgk], chan=("sgl", mm_ % 2, 0))
                        S.dma("sp", sg[:, 1, :], SG[32 + mm_, :, t0:t0 + T], writes=[(sgk, 1)], chan=("sgl", mm_ % 2, 1))
                        bankA, bkA = next_bank()
                        for hh in range(16):
                            S.op("pe", lambda: nc.tensor.matmul(bankA[:, :T], lhsT=wva[:, hh, 128 * m:128 * m + 128], rhs=a_sb[:, hh, :], start=(hh == 0), stop=(hh == 15)),
                                 reads=[wka] + akeys, writes=[bkA], sig=(hh == 15))
                        bankB, bkB = next_bank()
                        for hh in range(16):
                            S.op("pe", lambda: nc.tensor.matmul(bankB[:, :T], lhsT=wvb[:, hh, 128 * m:128 * m + 128], rhs=b_sb[:, hh, :], start=(hh == 0), stop=(hh == 15)),
                                 reads=[wkb] + bkeys_, writes=[bkB], sig=(hh == 15))
                        S.op("dve", lambda: nc.vector.tensor_tensor(out=t1[:], in0=bankA[:, :T], in1=sg[:, 0, :], op=ALU.mult), reads=[bkA, sgk], writes=["t1c"])
                        S.op("dve", lambda: nc.vector.tensor_tensor(out=t2[:], in0=bankB[:, :T], in1=sg[:, 1, :], op=ALU.mult), reads=[bkB, (sgk, 1)], writes=["t2c"])
                        S.op("dve", lambda: nc.vector.tensor_tensor(out=mT[:, mm_, :], in0=t1[:], in1=t2[:], op=ALU.add), reads=["t1c", "t2c"], writes=["mT"])
                for n in range(8):
                    wv, wk = load_w(wslots, ("wo", n), 32, 512)
                    for (off, nn) in blocks_of(T):
                        xb, xbk = xbs[xrr[0] % 3]
                        xrr[0] += 1
                        S.dma("sp", xb[:nn, :], xw[t0 + off:t0 + off + nn, 512 * n:512 * n + 512], writes=[xbk])
                        bank, bkey = t_proj(wv, wk, 32, 0, 512, mT, "mT", off, nn)
                        S.op("dve", lambda: nc.vector.tensor_tensor(out=xb[:nn, :], in0=bank[:nn, :], in1=xb[:nn, :], op=ALU.add), reads=[bkey, xbk], writes=[xbk])
                        S.dma("pool", X1[t0 + off:t0 + off + nn, 512 * n:512 * n + 512], xb[:nn, :], reads=[xbk], writes=["X1"], chan=("xbst", (xrr[0] - 1) % 3))
            S.barrier()

    if "D" in phases:
        with ExitStack() as stD:
            xts = [(S.sb("xtD", [128, D], F32, stD), ("xt", 0))]
            junk = S.sb("junkD", [128, D], BF16, stD)
            grep = S.sb("grep", [128, D], F32, stD)
            h2T = S.sb("h2T", [128, 32, 464], BF16, stD)
            actT = S.sb("actT", [128, 43, 464], BF16, stD)
            wslots = [(S.sb("wsD%d" % k, [128, 32 * 256], BF16, stD), ("wslot", k)) for k in range(3)]
            dslots = [(S.sb("wdD%d" % k, [128, 8 * 512], BF16, stD), ("dslot", k)) for k in range(2)]
            cg_ = S.sb("cg_", [128, 464], F32, stD)
            cu_ = S.sb("cu_", [128, 464], F32, stD)
            cs_ = S.sb("cs_", [128, 464], F32, stD)
            xbs = [(S.sb("xbD%d" % k, [128, 512], F32, stD), ("xb", k)) for k in range(3)]
            ssqp = S.sb("ssqp", [128, 4, 8], F32, stD)
            rsf = S.sb("rsf", [128, 4], F32, stD)
            xrr = [0]
            S.dma("sp", grep[:], final_norm_g.partition_broadcast(128), writes=["grep"])
            S.op("dve", lambda: nc.vector.memset(actT[:], 0.0), writes=["actT"])
            for i in range(n_own_tiles):
                r0 = T * i
                Tc = min(T + 2, WIN - r0)
                make_hT(X1[r0:r0 + Tc, :], Tc, g_ffn, "g_ffn", h2T, "h2T", xts, extra_read=["X1"])
                if i == 0:
                    S.op("dve", lambda: nc.vector.tensor_scalar(out=h2T[:, :, 0:1], in0=h2T[:, :, 0:1], scalar1=cvals[:, 34:35], scalar2=None, op0=ALU.mult),
                         reads=["h2T", "cvals"], writes=["h2T"])
                if i == 8:
                    S.op("dve", lambda: nc.vector.tensor_scalar(out=h2T[:, :, 449:450], in0=h2T[:, :, 449:450], scalar1=cvals[:, 35:36], scalar2=None, op0=ALU.mult),
                         reads=["h2T", "cvals"], writes=["h2T"])
                W2 = Tc - 2
                for g in range(2):
                    for fl in range(43):
                        f = 43 * g + fl
                        wv, wk = load_w(wslots, ("up", f), 32, 256)
                        bankG, bkG = f_proj(wv, wk, 32, 0, 128, h2T, "h2T", Tc)
                        bankU, bkU = f_proj(wv, wk, 32, 128, 128, h2T, "h2T", Tc)
                        for (bank, bkey, dst, dk, cidx) in ((bankG, bkG, cg_, "cg_", f), (bankU, bkU, cu_, "cu_", 86 + f)):
                            S.op("dve", lambda: nc.vector.tensor_scalar(out=dst[:, 1:1 + W2], in0=bank[:, 1:1 + W2], scalar1=cw[:, 1, cidx:cidx + 1], scalar2=cb[:, cidx:cidx + 1],
                                                                        op0=ALU.mult, op1=ALU.add),
                                 reads=[bkey, ("cw", 1), "cb"], writes=[dk])
                            S.op("dve", lambda: nc.vector.scalar_tensor_tensor(out=dst[:, 1:1 + W2], in0=bank[:, 0:W2], scalar=cw[:, 0, cidx:cidx + 1], in1=dst[:, 1:1 + W2],
                                                                                op0=ALU.mult, op1=ALU.add),
                                 reads=[bkey, ("cw", 0), dk], writes=[dk])
                            S.op("dve", lambda: nc.vector.scalar_tensor_tensor(out=dst[:, 1:1 + W2], in0=bank[:, 2:2 + W2], scalar=cw[:, 2, cidx:cidx + 1], in1=dst[:, 1:1 + W2],
                                                                                op0=ALU.mult, op1=ALU.add),
                                 reads=[bkey, ("cw", 2), dk], writes=[dk])
                        S.op("act", lambda: nc.scalar.activation(out=cs_[:, 1:1 + W2], in_=cg_[:, 1:1 + W2], func=AF.Silu), reads=["cg_"], writes=["cs_"])
                        S.op("dve", lambda: nc.vector.tensor_tensor(out=actT[:, fl, 1:1 + W2], in0=cs_[:, 1:1 + W2], in1=cu_[:, 1:1 + W2], op=ALU.mult),
                             reads=["cs_", "cu_"], writes=["actT"])
                    cgs = [cg for cg in range(NCG_DOWN) if (cg * 8) // 43 == g or (min(cg * 8 + 7, 85)) // 43 == g]
                    for n in range(8):
                        banks = [next_bank() for _ in blocks_of(Tc)]
                        mmlist = []
                        for cg in cgs:
                            kc = min(8, 86 - cg * 8)
                            for c in range(kc):
                                f = cg * 8 + c
                                if f // 43 == g:
                                    mmlist.append((cg, c, f))
                        cur = None
                        for idx, (cg, c, f) in enumerate(mmlist):
                            if cur is None or cur[0] != cg:
                                kc = min(8, 86 - cg * 8)
                                s = wslot_rr[0] % 2
                                wslot_rr[0] += 1
                                wt, wkk = dslots[s]
                                wvv = wt[:, 0:kc * 512].rearrange("p (c m) -> p c m", c=kc)
                                S.dma("sp", wvv, wsc[("down", n, cg)], reads=[("ws", ("down", n, cg))], writes=[wkk])
                                cur = (cg, wvv, wkk)
                            lastcg = (idx == len(mmlist) - 1) or (mmlist[idx + 1][0] != cg)
                            blks = blocks_of(Tc)
                            for bi, (off, nn) in enumerate(blks):
                                bank, bkey = banks[bi]
                                S.op("pe", lambda: nc.tensor.matmul(bank[:nn, :], lhsT=actT[:, f - 43 * g, off:off + nn], rhs=cur[1][:, c, :],
                                                                    start=(idx == 0), stop=(idx == len(mmlist) - 1)),
                                     reads=["actT", cur[2]], writes=[bkey], sig=(idx == len(mmlist) - 1) or (lastcg and bi == len(blks) - 1))
                        for bi, (off, nn) in enumerate(blocks_of(Tc)):
                            bank, bkey = banks[bi]
                            xb, xbk = xbs[xrr[0] % 3]
                            xrr[0] += 1
                            src = X1 if g == 0 else X2
                            rb0 = r0 if g == 0 else 464 * i
                            S.dma("sp", xb[:nn, :], src[rb0 + off:rb0 + off + nn, 512 * n:512 * n + 512], reads=["X1", ("X2", i, n, bi)], writes=[xbk])
                            S.op("dve", lambda: nc.vector.tensor_tensor(out=xb[:nn, :], in0=bank[:nn, :], in1=xb[:nn, :], op=ALU.add), reads=[bkey, xbk], writes=[xbk])
                            if g == 1:
                                S.op("dve", lambda: nc.vector.scalar_tensor_tensor(out=junk[:nn, 0:512], in0=xb[:nn, :], scalar=1.0, in1=xb[:nn, :], op0=ALU.mult, op1=ALU.mult,
                                                                                    accum_out=ssqp[:nn, bi, n:n + 1]),
                                     reads=[xbk], writes=["junk", ("ssqp", bi, n)])
                            S.dma("pool", X2[464 * i + off:464 * i + off + nn, 512 * n:512 * n + 512], xb[:nn, :], reads=[xbk], writes=[("X2", i, n, bi)],
                                  chan=("xbst", (xrr[0] - 1) % 3))
                for bi, (off, nn) in enumerate(blocks_of(Tc)):
                    xs, xkey = xts[0]
                    S.dma("sp", xs[:nn, :], X2[464 * i + off:464 * i + off + nn, :], reads=[("X2", i, n, bi) for n in range(8)], writes=[xkey])
                    S.op("dve", lambda: nc.vector.tensor_reduce(out=rsf[:nn, bi:bi + 1], in_=ssqp[:nn, bi, :], axis=mybir.AxisListType.X, op=ALU.add),
                         reads=[("ssqp", bi, n) for n in range(8)], writes=[("rsf", bi)])
                    S.op("act", lambda: nc.scalar.activation(out=rsf[:nn, bi:bi + 1], in_=rsf[:nn, bi:bi + 1], func=AF.Ln, scale=1.0 / D, bias=epsv[:nn, :]),
                         reads=[("rsf", bi), "epsv"], writes=[("rsf", bi)])
                    S.op("act", lambda: nc.scalar.activation(out=rsf[:nn, bi:bi + 1], in_=rsf[:nn, bi:bi + 1], func=AF.Exp, scale=-0.5),
                         reads=[("rsf", bi)], writes=[("rsf", bi)])
                    S.op("dve", lambda: nc.vector.scalar_tensor_tensor(out=xs[:nn, :], in0=xs[:nn, :], scalar=rsf[:nn, bi:bi + 1], in1=grep[:nn, :], op0=ALU.mult, op1=ALU.mult),
                         reads=[xkey, ("rsf", bi), "grep"], writes=[xkey])
                    a = max(off, 1)
                    b = min(off + nn, T + 1, 4097 - r0)
                    if b > a:
                        S.dma("pool", y_out[r0 + a - 1:r0 + b - 1, :], xs[a - off:b - off, :], reads=[xkey], writes=[("y", i, bi)], chan="yout")
            S.barrier()

    S.finish()
    return nc, S


def _t5_bucket(rel):
    rel = np.asarray(rel, np.int32)
    nb, me = 16, 8
    ret = (rel > 0).astype(np.int32) * nb
    n = np.abs(rel)
    nf = np.maximum(n, me).astype(np.float32)
    large = me + (np.log(nf / np.float32(me)) / np.float32(math.log(128 / 8)) * np.float32(nb - me)).astype(np.int32)
    large = np.minimum(large, nb - 1)
    return ret + np.where(n < me, n, large)


def _rope_tables(pos):
    inv_freq = (np.float32(10000.0) ** (-np.arange(0, 64, 2, dtype=np.float32) / np.float32(64))).astype(np.float32)
    ang = pos.astype(np.float32)[None, :] * inv_freq[:, None]
    c = np.cos(ang).astype(np.float32)
    s = np.sin(ang).astype(np.float32)
    return np.concatenate([c, c], 0), np.concatenate([-s, s], 0)


def core_inputs(core, x_prompt, x_sample):
    if core < 4:
        own = x_prompt[core]
        oth = np.zeros((NOTH, D), np.float32)
        Lrow = np.zeros(D, np.float32)
        Rrow = np.zeros(D, np.float32)
        own_start, oth_start, typ = 0, 4096, "C"
    else:
        s, half = (core - 4) // 2, (core - 4) % 2
        own = x_sample[s, 4096 * half:4096 * half + 4096]
        oth = x_sample[s, 4096 * (1 - half):4096 * (1 - half) + 4096]
        own_start, oth_start = 4096 * half, 4096 * (1 - half)
        if half == 0:
            Lrow = np.zeros(D, np.float32); Rrow = x_sample[s, 4096]; typ = "A"
        else:
            Lrow = x_sample[s, 4095]; Rrow = np.zeros(D, np.float32); typ = "B"
    xw = np.zeros((WIN, D), np.float32)
    xw[0] = Lrow
    xw[1:4097] = own
    xw[4097] = Rrow
    wpos = own_start - 1 + np.arange(WIN)
    opos = oth_start + np.arange(NOTH)
    cs1w, cs2w = _rope_tables(wpos)
    cs1o, cs2o = _rope_tables(opos)
    p = np.arange(128)[:, None]
    u = np.arange(WT)[None, :]
    bidx = np.zeros((128, WB), np.float32)
    bidx[:, :WT] = _t5_bucket(p - u + U0)
    tq = np.arange(T)[None, :]
    if typ == "A":
        bidx[:, WT:WT + T] = _t5_bucket((4096 + p) - (3647 + tq))
        bidx[:, WT + T:] = 31
        sel, cm, mL, mR = 31, 0.0, 0.0, 1.0
    elif typ == "B":
        bidx[:, WT:WT + T] = 15
        bidx[:, WT + T:] = _t5_bucket((3968 - 4096 + p) - (-1 + tq))
        sel, cm, mL, mR = 15, 0.0, 1.0, 0.0
    else:
        bidx[:, WT:] = 32
        sel, cm, mL, mR = 32, NEG, 0.0, 0.0
    cvals = np.zeros((128, 40), np.float32)
    cvals[:, sel] = 1.0
    cvals[:, 33] = cm
    cvals[:, 34] = mL
    cvals[:, 35] = mR
    return {"xw": xw, "xo": np.ascontiguousarray(oth), "cs1w": cs1w, "cs2w": cs2w, "cs1o": cs1o, "cs2o": cs2o,
            "bidx": bidx, "cvals": cvals, "ident": np.eye(128, dtype=np.float32)}


WEIGHT_NAMES = ["rel_bias", "final_norm_g", "rms_attn_g", "w_in", "lambda_q1", "lambda_k1", "lambda_q2", "lambda_k2",
                "diff_subln_g", "mla_q_norm_g", "w_mla_q_up", "mla_kv_norm_g", "w_mla_kv_up", "w_branch_a",
                "w_branch_b", "w_out", "rms_ffn_g", "w_ffn_up", "conv_w", "conv_b", "w_ffn_down"]


def kernel(**inputs):
    x_prompt = np.asarray(inputs["x_prompt"], np.float32)
    x_sample = np.asarray(inputs["x_sample"], np.float32)
    shared = {k: np.ascontiguousarray(np.asarray(inputs[k], np.float32)) for k in WEIGHT_NAMES}
    nc, _ = build_program()
    in_maps = []
    for core in range(8):
        m = dict(shared)
        m.update(core_inputs(core, x_prompt, x_sample))
        in_maps.append(m)
    res = run_bass_kernel_spmd(nc, in_maps, core_ids=list(range(8)))
    ys = [np.asarray(r["y"], np.float32) for r in res.results]
    y_prompt = np.stack(ys[0:4], 0)
    y_sample = np.stack([np.concatenate([ys[4], ys[5]], 0), np.concatenate([ys[6], ys[7]], 0)], 0)
    return (y_prompt, y_sample)
```
